# Optimizing a Trainium2 kernel written in Bass

```python
import jax, jax.numpy as jnp
from jax import lax
import numpy as np

D_MODEL = 1024
BATCH = 8
SEQ = 2048
DEPTH = 4
DEC_BATCH = 128
DEC_SEQ = 4
PAST_LEN = 2048
PAGE_SIZE = 128

CHUNK = 128
A_GROUPS = 4
A_GROUP_W = 128
A_WIDTH = A_GROUPS * A_GROUP_W
DIL_PAIRS = ((128, 1), (512, 4), (2048, 16))
N_DIL = len(DIL_PAIRS)
HEADS_PER_GROUP = 4
HEAD_DIM = 64
B_GROUP_W = HEADS_PER_GROUP * HEAD_DIM
B_QKV_W = 3 * N_DIL * B_GROUP_W
BAND_BLOCK = 128
D_FF = 2816
CONV_W = 3
IN_W = 2 * A_WIDTH + B_QKV_W + 2 * D_MODEL
EPS = 1e-6
NEG = -1e30

kernel_name = 'hybrid_gmlp_dilated_swa_convglu_step'


def rms_norm(x, g):
    xf = x.astype(jnp.float32)
    y = xf * lax.rsqrt(jnp.mean(xf * xf, axis=-1, keepdims=True) + EPS)
    return (y * g.astype(jnp.float32)).astype(x.dtype)


def layer_norm(x, g, b):
    xf = x.astype(jnp.float32)
    mu = jnp.mean(xf, axis=-1, keepdims=True)
    var = jnp.mean(jnp.square(xf - mu), axis=-1, keepdims=True)
    y = (xf - mu) * lax.rsqrt(var + EPS)
    return (y * g.astype(jnp.float32) + b.astype(jnp.float32)).astype(x.dtype)


def spatial_gate(u, vn, w_s, b_s):
    N, T, _ = u.shape
    L = min(T, CHUNK)
    nc = T // L
    w = jnp.tril(w_s[:, :L, :L])
    vg = vn.reshape(N, nc, L, A_GROUPS, A_GROUP_W)
    mix = jnp.einsum('gts,bnsgc->bntgc', w, vg) + b_s[:, :L].T[None, None, :, :, None]
    return jax.nn.gelu(u) * mix.reshape(N, T, A_WIDTH)


def dilated_band_attn(q, k, v, dil, span):
    N, T, H, Dh = q.shape
    n = T // dil
    nb = -(-n // BAND_BLOCK)
    npad = nb * BAND_BLOCK

    def to_sub(a):
        a = a.reshape(N, n, dil, H, Dh).transpose(0, 2, 1, 3, 4)
        return jnp.pad(a, ((0, 0), (0, 0), (0, npad - n), (0, 0), (0, 0)))

    def band(a):
        a = jnp.pad(a, ((0, 0), (0, 0), (BAND_BLOCK, 0), (0, 0), (0, 0)))
        a = a.reshape(N, dil, nb + 1, BAND_BLOCK, H, Dh)
        return jnp.concatenate([a[:, :, :-1], a[:, :, 1:]], axis=3)

    qb = to_sub(q).reshape(N, dil, nb, BAND_BLOCK, H, Dh)
    kb = band(to_sub(k))
    vb = band(to_sub(v))
    s = jnp.einsum('brnqhd,brnkhd->brnhqk', qb, kb).astype(jnp.float32) * (HEAD_DIM ** -0.5)
    qi = jnp.arange(BAND_BLOCK)[:, None]
    ki = jnp.arange(2 * BAND_BLOCK)[None, :]
    dist = qi + BAND_BLOCK - ki
    blk = jnp.arange(nb)[:, None, None]
    valid = (dist >= 0) & (dist <= span) & (blk * BAND_BLOCK + ki - BAND_BLOCK >= 0)
    s = jnp.where(valid[None, None, :, None], s, NEG)
    lse = jax.nn.logsumexp(s, axis=-1)
    p = jnp.exp(s - lse[..., None])
    o = jnp.einsum('brnhqk,brnkhd->brnqhd', p.astype(v.dtype), vb)
    o = o.reshape(N, dil, npad, H, Dh)[:, :, :n].transpose(0, 2, 1, 3, 4).reshape(N, T, H, Dh)
    lse = lse.transpose(0, 1, 2, 4, 3).reshape(N, dil, npad, H)[:, :, :n]
    lse = lse.transpose(0, 2, 1, 3).reshape(N, T, H)
    return o, lse


def dilated_cached_attn(q, k_new, v_new, kv_buf, dil, span):
    N, Tn, H, Dh = q.shape
    Wb = kv_buf.shape[1]
    keys = jnp.concatenate([kv_buf[:, :, 0], k_new], axis=1)
    vals = jnp.concatenate([kv_buf[:, :, 1], v_new], axis=1)
    idx = Wb + jnp.arange(Tn)[:, None] - dil * jnp.arange(span + 1)[None, :]
    valid = idx >= 0
    idx = jnp.maximum(idx, 0)
    kg = keys[:, idx]
    vg = vals[:, idx]
    s = jnp.einsum('bqhd,bqjhd->bhqj', q, kg).astype(jnp.float32) * (HEAD_DIM ** -0.5)
    s = jnp.where(valid[None, None], s, NEG)
    lse = jax.nn.logsumexp(s, axis=-1)
    p = jnp.exp(s - lse[..., None])
    o = jnp.einsum('bhqj,bqjhd->bqhd', p.astype(v_new.dtype), vg)
    return o, lse.transpose(0, 2, 1)


def combine_dilations(outs, lses):
    w = jax.nn.softmax(jnp.stack(lses, axis=0), axis=0)
    o = jnp.stack(outs, axis=0)
    return jnp.einsum('gnth,gnthd->nthd', w.astype(o.dtype), o)


def conv_glu(h, w_up, conv_w, conv_b, w_down, conv_state):
    T = h.shape[1]
    up = h @ w_up
    gate, val = up[..., :D_FF], up[..., D_FF:]
    hp = jnp.concatenate([conv_state, gate], axis=1)
    conv = conv_b + sum(conv_w[j] * hp[:, j:j + T] for j in range(CONV_W))
    out = (jax.nn.gelu(conv) * val) @ w_down
    return out, hp[:, -(CONV_W - 1):]


def trunk_layer(x, c, lw, kv_bufs, conv_state):
    (ada_w, ada_b, norm_g, w_in, ln_v_g, ln_v_b, w_spatial, b_spatial,
     w_a2d, w_b2d, w_out, w_up, conv_w, conv_b, w_down) = lw
    N, T, _ = x.shape
    mod = (jax.nn.silu(c) @ ada_w + ada_b)[:, None, :]
    sh1, sc1, gt1, sh2, sc2, gt2 = jnp.split(mod, 6, axis=-1)

    h = rms_norm(x, norm_g[0]) * (1 + sc1) + sh1
    p = h @ w_in
    o0 = 2 * A_WIDTH
    o1 = o0 + B_QKV_W
    u = p[..., :A_WIDTH]
    v = p[..., A_WIDTH:o0]
    qkv = p[..., o0:o1].reshape(N, T, 3, N_DIL, HEADS_PER_GROUP, HEAD_DIM)
    g_a = jax.nn.sigmoid(p[..., o1:o1 + D_MODEL])
    g_b = jax.nn.sigmoid(p[..., o1 + D_MODEL:])

    vn = layer_norm(jax.nn.gelu(v), ln_v_g, ln_v_b)
    y_a = spatial_gate(u, vn, w_spatial, b_spatial)

    outs, lses, kv_rows = [], [], []
    for gi, (win, dil) in enumerate(DIL_PAIRS):
        q, k, vv = qkv[:, :, 0, gi], qkv[:, :, 1, gi], qkv[:, :, 2, gi]
        if kv_bufs is None:
            o, lse = dilated_band_attn(q, k, vv, dil, win // dil)
            keep = min(win, T)
            kv_rows.append(jnp.stack([k[:, T - keep:], vv[:, T - keep:]], axis=2))
        else:
            o, lse = dilated_cached_attn(q, k, vv, kv_bufs[gi], dil, win // dil)
            kv_rows.append(jnp.stack([k, vv], axis=2))
        outs.append(o)
        lses.append(lse)
    y_b = combine_dilations(outs, lses).reshape(N, T, B_GROUP_W)

    merged = g_a * (y_a @ w_a2d) + g_b * (y_b @ w_b2d)
    x = x + gt1 * rms_norm(merged @ w_out, norm_g[1])

    h2 = rms_norm(x, norm_g[2]) * (1 + sc2) + sh2
    f, conv_new = conv_glu(h2, w_up, conv_w, conv_b, w_down, conv_state)
    x = x + gt2 * rms_norm(f, norm_g[3])
    return x, kv_rows, conv_new, vn


def setup_inputs(seed: int = 0):
    key = jax.random.key(seed)
    ks = jax.random.split(key, 32)
    f32 = jnp.float32
    D = D_MODEL

    def nrm(k, shape, scale):
        return jax.random.normal(k, shape, f32) * scale

    swa = [nrm(ks[2 + g], (DEPTH, DEC_BATCH, min(w, PAST_LEN), 2, HEADS_PER_GROUP, HEAD_DIM), 1.0)
           for g, (w, _) in enumerate(DIL_PAIRS)]
    return {
        'x_prompt': nrm(ks[0], (BATCH, SEQ, D), 1.0),
        'x_sample': nrm(ks[1], (DEC_BATCH, DEC_SEQ, D), 1.0),
        'cache_swa0': swa[0],
        'cache_swa1': swa[1],
        'cache_swa2': swa[2],
        'state_ffn_conv': nrm(ks[5], (DEPTH, DEC_BATCH, CONV_W - 1, D_FF), 1.0),
        'c_prompt': nrm(ks[6], (BATCH, D), 1.0),
        'c_sample': nrm(ks[7], (DEC_BATCH, D), 1.0),
        'ada_w': nrm(ks[8], (DEPTH, D, 6 * D), 0.5 * D ** -0.5),
        'ada_b': nrm(ks[9], (DEPTH, 6 * D), 0.01),
        'norm_g': 1.0 + nrm(ks[10], (DEPTH, 4, D), 0.05),
        'w_in': nrm(ks[11], (DEPTH, D, IN_W), D ** -0.5),
        'ln_v_g': 1.0 + nrm(ks[12], (DEPTH, A_WIDTH), 0.05),
        'ln_v_b': nrm(ks[13], (DEPTH, A_WIDTH), 0.02),
        'w_spatial': nrm(ks[14], (DEPTH, A_GROUPS, CHUNK, CHUNK), CHUNK ** -0.5),
        'b_spatial': 1.0 + nrm(ks[15], (DEPTH, A_GROUPS, CHUNK), 0.1),
        'w_a2d': nrm(ks[16], (DEPTH, A_WIDTH, D), A_WIDTH ** -0.5),
        'w_b2d': nrm(ks[17], (DEPTH, B_GROUP_W, D), B_GROUP_W ** -0.5),
        'w_out': nrm(ks[18], (DEPTH, D, D), D ** -0.5),
        'w_up': nrm(ks[19], (DEPTH, D, 2 * D_FF), D ** -0.5),
        'conv_w': nrm(ks[20], (DEPTH, CONV_W, D_FF), CONV_W ** -0.5),
        'conv_b': nrm(ks[21], (DEPTH, D_FF), 0.02),
        'w_down': nrm(ks[22], (DEPTH, D_FF, D), D_FF ** -0.5),
    }


def reference(x_prompt, x_sample, cache_swa0, cache_swa1, cache_swa2, state_ffn_conv, c_prompt, c_sample,
              ada_w, ada_b, norm_g, w_in, ln_v_g, ln_v_b, w_spatial, b_spatial, w_a2d, w_b2d, w_out,
              w_up, conv_w, conv_b, w_down):
    y_p, y_s = x_prompt, x_sample
    swa_p = [[] for _ in range(N_DIL)]
    swa_s = [[] for _ in range(N_DIL)]
    conv_p, conv_s, chunk_v_s = [], [], []
    zero_conv = jnp.zeros((x_prompt.shape[0], CONV_W - 1, D_FF), x_prompt.dtype)
    for l in range(DEPTH):
        lw = (ada_w[l], ada_b[l], norm_g[l], w_in[l], ln_v_g[l], ln_v_b[l], w_spatial[l], b_spatial[l],
              w_a2d[l], w_b2d[l], w_out[l], w_up[l], conv_w[l], conv_b[l], w_down[l])
        y_p, kv_rows_p, cst_p, _ = trunk_layer(y_p, c_prompt, lw, None, zero_conv)
        y_s, kv_rows_s, cst_s, vn_s = trunk_layer(
            y_s, c_sample, lw, (cache_swa0[l], cache_swa1[l], cache_swa2[l]), state_ffn_conv[l])
        for g in range(N_DIL):
            swa_p[g].append(kv_rows_p[g])
            swa_s[g].append(kv_rows_s[g])
        conv_p.append(cst_p)
        conv_s.append(cst_s)
        chunk_v_s.append(vn_s)
    new_swa0_prompt = jnp.stack(swa_p[0])
    new_swa1_prompt = jnp.stack(swa_p[1])
    new_swa2_prompt = jnp.stack(swa_p[2])
    new_conv_prompt = jnp.stack(conv_p)
    new_swa0_sample = jnp.stack(swa_s[0])
    new_swa1_sample = jnp.stack(swa_s[1])
    new_swa2_sample = jnp.stack(swa_s[2])
    new_conv_sample = jnp.stack(conv_s)
    new_chunk_v_sample = jnp.stack(chunk_v_s)
    return (y_p, y_s, new_swa0_prompt, new_swa1_prompt, new_swa2_prompt, new_conv_prompt,
            new_swa0_sample, new_swa1_sample, new_swa2_sample, new_conv_sample, new_chunk_v_sample)
```

```python
import numpy as np
from contextlib import ExitStack
import concourse.bass as bass
import concourse.mybir as mybir
from concourse.bass_utils import run_bass_kernel_spmd

F32 = mybir.dt.float32
BF16 = mybir.dt.bfloat16
AF = mybir.ActivationFunctionType
ALU = mybir.AluOpType

D = 1024
KC = 8
NP_ = 2048
NS = 64
NT = NP_ + NS
DEPTH = 4
IN_W = 5376
O0 = 1024
O1 = O0 + 2304
DFF = 2816
FC = 22
EPS = 1e-6
DILS = (1, 4, 16)
TILES = [(0, 512), (512, 512), (1024, 512), (1536, 512), (2048, 64)]
GROUPS = [[0, 1], [2, 3, 4]]
RING_UNITS = 11
RING_UNIT = 1024

V_ADAB = 0
V_NORMG = V_ADAB + 4 * 48
V_CONVW = V_NORMG + 4 * 32
V_CONVB = V_CONVW + 4 * 66
NV = V_CONVB + 4 * 22
C_ID = 0
C_MPREV = 128
C_MOWN = 256
C_SSP = 384
C_SNEW0 = 448
C_SNEW1 = 512
C_MC0 = 576
C_MC1 = 592
NCON = 656


class WS(list):
    whole = None


def _flat(xs):
    out = []
    for x in xs:
        if isinstance(x, (list, tuple)):
            out.extend(_flat(x))
        else:
            out.append(x)
    return out


class Buf:
    __slots__ = ("name", "w", "r", "sem", "cnt", "tot", "excl")

    def __init__(self, name):
        self.name = name
        self.excl = False
        self.w = None
        self.r = []
        self.sem = None
        self.cnt = 0
        self.tot = 0


class Op:
    __slots__ = ("eng", "fn", "deps", "sig", "sigidx", "key", "waits", "tag")


class Prog:
    ENG = ("pe", "act", "dve", "pool", "sp")

    def __init__(self):
        self.ops = {e: [] for e in self.ENG}
        self.all = []
        self.bar_ops = []
        self.bar_pending = set()
        self.dma_last = {}
        self.tag = ""
        self.last_bar = {}

    def barrier(self):
        ops = [self.last_bar[e] for e in self.ENG if e in self.last_bar]
        ops += list(self.dma_last.values())
        self.bar_ops = ops
        self.bar_pending = set(self.ENG)
        self.dma_last = {}

    def add(self, eng, fn, reads=(), writes=(), key=None, nobar=False):
        op = Op()
        op.eng = eng
        op.fn = fn
        op.key = key
        op.sig = False
        op.sigidx = 0
        op.waits = []
        op.tag = self.tag
        reads = _flat(reads)
        writes = _flat(writes)
        ex = [b for b in reads if b.excl]
        if ex:
            reads = [b for b in reads if not b.excl]
            writes = list(writes) + [b for b in ex if b not in writes]
        deps = []
        if not nobar:
            if eng in self.bar_pending:
                deps.extend(self.bar_ops)
                self.bar_pending.discard(eng)
            if key is not None:
                self.dma_last[id(key)] = op
            self.last_bar[eng] = op
        for b in reads:
            if b.w is not None:
                deps.append(b.w)
        for b in writes:
            if b.w is not None:
                deps.append(b.w)
            deps.extend(b.r)
        dd = []
        seen = set()
        for d in deps:
            if id(d) in seen or d is op:
                continue
            seen.add(id(d))
            if d.eng == "pe" and eng == "pe" and d.key is None and key is None:
                continue
            if key is not None and d.key is key and d.eng == eng:
                continue
            dd.append(d)
        op.deps = dd
        for b in reads:
            b.r = [r for r in b.r if not (r.eng == eng and (r.key is None) == (key is None) and key is None)]
            b.r.append(op)
        for b in writes:
            b.w = op
            b.r = []
        self.ops[eng].append(op)
        self.all.append(op)
        return op

    def finalize(self, nc, stack):
        for op in self.all:
            for d in op.deps:
                if d.key is None:
                    d.sig = True
        self.sems = {}
        for e in self.ENG:
            self.sems[e] = stack.enter_context(nc.semaphore("sem_" + e))
            c = 0
            for op in self.ops[e]:
                if op.sig and op.key is None:
                    c += 1
                    op.sigidx = c
        waited = {e: {} for e in self.ENG}
        keys = []
        for op in self.all:
            w = waited[op.eng]
            for d in op.deps:
                if d.key is None:
                    sid, sem, val = ("E", d.eng), self.sems[d.eng], d.sigidx
                else:
                    k = d.key
                    if k.sem is None:
                        k.sem = stack.enter_context(nc.semaphore("sem_d%d" % len(keys)))
                        keys.append(k)
                    sid, sem, val = ("K", id(k)), k.sem, 16 * k.cnt
                if w.get(sid, 0) >= val:
                    continue
                w[sid] = val
                op.waits.append((sem, val))
            if op.key is not None:
                k = op.key
                if k.sem is None:
                    k.sem = stack.enter_context(nc.semaphore("sem_d%d" % len(keys)))
                    keys.append(k)
                k.cnt += 1
        self.keys = keys

    def emit(self, eng_name, eng, final_waits=()):
        sem_e = self.sems[eng_name]
        for op in self.ops[eng_name]:
            for sem, val in op.waits:
                eng.wait_ge(sem, val)
            ins = op.fn(eng)
            if op.key is not None:
                ins.then_inc(op.key.sem, 16)
            elif op.sig:
                ins.then_inc(sem_e, 1)
        for sem, val in final_waits:
            eng.wait_ge(sem, val)


def build(depth=DEPTH, stop=None):
    nc = bass.Bass("TRN2", target_bir_lowering=False)
    P = Prog()
    stack = ExitStack()

    def din(name, shape):
        return nc.dram_tensor(name, list(shape), F32, kind="ExternalInput").ap()

    def dout(name, shape):
        return nc.dram_tensor(name, list(shape), F32, kind="ExternalOutput").ap()

    xT_d = din("xT", [D, NT])
    cT_d = din("cT", [128, KC * 17])
    cache_d = [din("cache0", [DEPTH, 16, 128, 512]), din("cache1", [DEPTH, 16, 512, 512]),
               din("cache2", [DEPTH, 16, 2048, 512])]
    convst_d = din("convst", [DEPTH, 128, FC * 32])
    ada_w_d = din("ada_w", [DEPTH, D, 6 * D])
    w_in_d = din("w_in", [DEPTH, D, IN_W])
    w_a2d_d = din("w_a2d", [DEPTH, 512, D])
    w_b2d_d = din("w_b2d", [DEPTH, 256, D])
    w_out_d = din("w_out", [DEPTH, D, D])
    w_up_d = din("w_up", [DEPTH, D, 2 * DFF])
    w_down_d = din("w_down", [DEPTH, DFF, D])
    vecs_d = din("vecs", [128, NV])
    lnv_d = din("lnv", [DEPTH, 128, 1792])
    wspT_d = din("wspT", [DEPTH, 128, 512])
    wsps_d = din("wsps", [DEPTH, 64, 256])
    consts_d = din("consts", [128, NCON])

    yT_d = dout("yT", [D, NT])
    kvp_d = [dout("kvp0", [DEPTH, 128, 512]), dout("kvp1", [DEPTH, 512, 512]), dout("kvp2", [DEPTH, 2048, 512])]
    convp_d = dout("convp", [DEPTH, 128, FC * 2])
    kvs_d = dout("kvs", [DEPTH, 3, 64, 512])
    convs_d = dout("convs", [DEPTH, 128, FC * 32])
    vns_d = dout("vns", [DEPTH, 64, 512])

    def sb(name, shape, dt):
        return stack.enter_context(nc.sbuf_tensor(name, list(shape), dt))

    xT = sb("xT_sb", [128, KC, NT], F32)
    hT = sb("hT_sb", [128, KC * NT], BF16)
    AR1 = 17280
    arena = sb("arena", [128, AR1], BF16)
    AR2 = 29184 // 2
    arena2 = sb("arena2", [128, AR2], BF16)
    ring = sb("ring", [128, RING_UNITS * RING_UNIT], BF16)
    vecs = sb("vecs_sb", [128, NV], F32)
    constb = sb("constb", [128, NCON], BF16)
    ones = sb("ones", [128, 128], BF16)
    scT = sb("scT", [128, KC, 17], BF16)
    shT = sb("shT", [128, 2, KC, 17], F32)
    gs1 = sb("gs1", [128, KC, 17], F32)
    gg1 = sb("gg1", [128, KC, 17], F32)
    gs2 = sb("gs2", [128, KC, 17], F32)
    gg2 = sb("gg2", [128, KC, 17], F32)
    smallreg = sb("smallreg", [128, 6656], BF16)
    actx = sb("actx", [128, 1088], BF16)
    lnv = smallreg[:, 0:3584].bitcast(F32)
    wspb = smallreg[:, 3584:4096]
    wssb = smallreg[0:64, 4096:4352]
    qsT = smallreg[:, 4352:4736].rearrange("p (a q) -> p a q", a=6)
    ksT = smallreg[:, 4736:5120].rearrange("p (a q) -> p a q", a=6)
    qblk = smallreg[:, 5120:5888].rearrange("p (a b c) -> p a b c", a=6, b=16)
    vs_b = smallreg[0:64, 5888:6656].rearrange("p (a c) -> p a c", a=6)
    kvs_st = sb("kvs_st", [64, 256], F32)
    halo = sb("halo", [128, FC, 2], F32)
    convp_st = sb("convp_st", [128, FC, 2], F32)
    small = sb("small", [128, 16], F32)

    banks = [stack.enter_context(nc.psum_tensor("ps%d" % i, [128, 512], F32)) for i in range(8)]
    bankB = [Buf("bank%d" % i) for i in range(8)]
    for b_ in bankB:
        b_.excl = True
    bstate = {"next": 0, "pinned": set()}

    def nbank(pin=False):
        while True:
            i = bstate["next"]
            bstate["next"] = (i + 1) % 8
            if i not in bstate["pinned"]:
                break
        if pin:
            bstate["pinned"].add(i)
        return i

    def unpin(i):
        bstate["pinned"].discard(i)

    cT = arena2[:, 2048:2320].bitcast(F32).rearrange("p (c j) -> p c j", c=KC)
    modT = arena2[:, 0:1632].bitcast(F32).rearrange("p (m j) -> p m j", j=17)
    hTv = hT[:].rearrange("p (c t) -> p c t", c=KC)
    yaT = arena[:, 0:4 * NT].rearrange("p (c t) -> p c t", c=4)
    ybT = arena[:, 4 * NT:6 * NT].rearrange("p (c t) -> p c t", c=2)
    MG0 = 6 * NT
    GMAX = 1088
    def merged_c(m):
        if m < 4:
            return arena[:, MG0 + m * GMAX:MG0 + (m + 1) * GMAX]
        return smallreg[:, (m - 4) * GMAX:(m - 3) * GMAX]

    def act_c(c):
        if c < 15:
            return arena[:, c * GMAX:(c + 1) * GMAX]
        if c < 21:
            return smallreg[:, (c - 15) * GMAX:(c - 14) * GMAX]
        return actx[:, :]
    TA = MG0
    uacc = arena[:, TA:TA + 4096].bitcast(F32)
    qT = arena2[:, 0:2048]
    kT = arena2[:, 2048:4096]
    vtok = arena2[:, 4096:6144].rearrange("p (b c) -> p b c", b=16)
    A2T = 6144
    ug = arena2[:, A2T:A2T + 2048].rearrange("p (c t) -> p c t", c=4)
    vg_f = arena2[:, A2T + 2048:A2T + 3072].bitcast(F32)
    z_f = arena2[:, A2T + 3072:A2T + 4096].bitcast(F32)
    vn_f = arena2[:, A2T + 4096:A2T + 5120].bitcast(F32)
    vn_b = arena2[:, A2T + 5120:A2T + 5632]
    pex = [arena2[:, A2T + i * 256:A2T + (i + 1) * 256] for i in range(4)]
    pT = [arena2[:, A2T + 1024 + i * 256:A2T + 1024 + (i + 1) * 256] for i in range(4)]
    kvst = [arena2[:, A2T + 2048 + i * 512:A2T + 2048 + (i + 1) * 512].bitcast(F32) for i in range(2)]
    zacc = arena2[:, A2T + 3072:A2T + 3072 + 4096].bitcast(F32)
    NKV = 5
    kvc = [arena2[:, i * 2048:(i + 1) * 2048].rearrange("p (t c) -> p t c", c=512) for i in range(NKV)]
    KB0 = NKV * 2048
    kcT = [arena2[:, KB0 + i * 1024:KB0 + (i + 1) * 1024] for i in range(2)]
    pcs = [arena2[:, KB0 + 2048 + i * 64:KB0 + 2048 + (i + 1) * 64] for i in range(2)]
    pcm = [arena2[:, KB0 + 2176 + i * 64:KB0 + 2176 + (i + 1) * 64] for i in range(2)]
    pnew = arena2[:, KB0 + 2304:KB0 + 2304 + 3 * 256].rearrange("p (g h q) -> p g h q", g=3, h=4)
    pnx = arena2[:, KB0 + 3072:KB0 + 3072 + 256]
    rzs = arena2[:, KB0 + 3328:KB0 + 3328 + 256].bitcast(F32)
    assert KB0 + 3584 <= AR2
    NT0 = AR2 - 4096
    normD = (arena2[:, NT0:NT0 + 1024].rearrange("p (s t) -> p s t", s=2),
             arena2[:, NT0 + 1024:NT0 + 3072].bitcast(F32).rearrange("p (s t) -> p s t", s=2),
             arena2[:, NT0 + 3072:NT0 + 4096].bitcast(F32))
    normC = (arena2[:, 8192:9216].rearrange("p (s t) -> p s t", s=2),
             arena2[:, 10240:12288].bitcast(F32).rearrange("p (s t) -> p s t", s=2),
             arena2[:, 9216:10240].bitcast(F32))
    ytile_c = arena2[:, 0:8192].bitcast(F32).rearrange("p (c t) -> p c t", c=KC)
    ga_t = arena2[:, 8192:9216].bitcast(F32)
    gb_t = arena2[:, 9216:10240].bitcast(F32)
    t1_t = arena2[:, 10240:11264].bitcast(F32)
    t2_t = arena2[:, 11264:12288].bitcast(F32)
    H2N = KC * GMAX
    h2T = hT[:, 0:H2N].rearrange("p (c t) -> p c t", c=KC)
    ytile_d = hT[:, H2N:H2N + 8192].bitcast(F32).rearrange("p (c t) -> p c t", c=KC)
    GB = 2 + GMAX
    gbuf = [arena2[:, i * 2 * GB:(i + 1) * 2 * GB].bitcast(F32) for i in range(2)]
    acc_t = [arena2[:, 4 * GB + i * 1024:4 * GB + (i + 1) * 1024].bitcast(F32) for i in range(2)]
    ge_t = [arena2[:, 4 * GB + 2048 + i * 1024:4 * GB + 2048 + (i + 1) * 1024].bitcast(F32) for i in range(2)]
    gbs = [arena2[:, 4 * GB + 4096 + i * 192:4 * GB + 4096 + (i + 1) * 192].bitcast(F32).rearrange("p (j b) -> p j b", j=6) for i in range(2)]
    assert 4 * GB + 4096 + 384 <= NT0

    B = {}

    def bf(name):
        if name not in B:
            B[name] = Buf(name)
        return B[name]

    xB = [[bf("x%d_%d" % (c, t)) for t in range(5)] for c in range(KC)]
    hB = [[bf("h%d_%d" % (c, t)) for t in range(5)] for c in range(KC)]
    yaB = [bf("ya%d" % j) for j in range(17)]
    ybB = [[bf("yb%d_%d" % (p, s)) for s in range(2)] for p in range(2)]
    ringB = [bf("ring%d" % i) for i in range(RING_UNITS)]
    xload = bf("xload")
    misc_ld = bf("misc_ld")
    ystore = bf("ystore")
    out_keys = []

    def act(out, in_, func, reads, writes, bias=None, scale=None):
        kw = {}
        if bias is not None:
            kw["bias"] = bias
        if scale is not None:
            kw["scale"] = scale
        return P.add("act", lambda e: e.activation(out=out, in_=in_, func=func, **kw), reads, writes)

    def tt(eng, out, in0, in1, op, reads, writes):
        return P.add(eng, lambda e: e.tensor_tensor(out=out, in0=in0, in1=in1, op=op), reads, writes)

    def ts(eng, out, in0, s1, s2, op0, op1, reads, writes):
        if s2 is None:
            return P.add(eng, lambda e: e.tensor_scalar(out=out, in0=in0, scalar1=s1, scalar2=None, op0=op0), reads, writes)
        return P.add(eng, lambda e: e.tensor_scalar(out=out, in0=in0, scalar1=s1, scalar2=s2, op0=op0, op1=op1), reads, writes)

    def stt(out, in0, scalar, in1, op0, op1, reads, writes):
        return P.add("dve", lambda e: e.scalar_tensor_tensor(out=out, in0=in0, scalar=scalar, in1=in1, op0=op0, op1=op1), reads, writes)

    def cp(eng, out, in_, reads, writes):
        if eng == "act":
            return P.add("act", lambda e: e.copy(out=out, in_=in_), reads, writes)
        return P.add(eng, lambda e: e.tensor_copy(out=out, in_=in_), reads, writes)

    def mm(out, lhsT, rhs, start, stop, reads, bank, skip=False):
        return P.add("pe", lambda e: e.matmul(out, lhsT, rhs, start=start, stop=stop, skip_group_check=skip),
                     reads, [bankB[bank]])

    def dma(eng, out, in_, reads, writes, key, nobar=False):
        return P.add(eng, lambda e: e.dma_start(out=out, in_=in_), reads, writes, key=key, nobar=nobar)

    jobs = []
    mstate = {}

    tagbox = ["start"]

    def job(parts, fn):
        jobs.append((parts, fn, tagbox[0]))

    def settag(t):
        tagbox[0] = t

    def run_jobs():
        wjobs = [i for i, (p, f, _) in enumerate(jobs) if p is not None]
        pos = {j: n for n, j in enumerate(wjobs)}
        views = {}
        allocs = {}
        state = {"head": 0, "issued": 0}

        def size_of(parts):
            return sum(kc * ncol for (_, kc, ncol) in parts)

        def try_issue(cur):
            n = state["issued"]
            if n >= len(wjobs):
                return False
            j = wjobs[n]
            parts = jobs[j][0]
            nu = -(-size_of(parts) // RING_UNIT)
            assert nu <= RING_UNITS
            head = state["head"]
            if head + nu > RING_UNITS:
                head = 0
            units = set(range(head, head + nu))
            for m in range(cur, n):
                if allocs[m] & units:
                    return False
            allocs[n] = units
            state["head"] = head + nu
            state["issued"] = n + 1
            bufs = [ringB[u] for u in sorted(units)]
            key = bufs[0]
            base = head * RING_UNIT
            vs = WS()
            kcs = set(p[1] for p in parts)
            off = 0
            if len(kcs) == 1:
                kc = parts[0][1]
                tot = sum(p[2] for p in parts)
                whole = ring[:, base:base + kc * tot].rearrange("p (k n) -> p k n", k=kc)
                for (src, _, ncol) in parts:
                    dst = whole[:, :, off:off + ncol]
                    dma("pool", dst, src, [], bufs, key, nobar=True)
                    vs.append(dst)
                    off += ncol
                vs.whole = whole
            else:
                for (src, kc, ncol) in parts:
                    dst = ring[:, base + off:base + off + kc * ncol].rearrange("p (k n) -> p k n", k=kc)
                    dma("pool", dst, src, [], bufs, key, nobar=True)
                    vs.append(dst)
                    off += kc * ncol
            views[j] = (vs, bufs)
            return True

        for i, (parts, fn, _) in enumerate(jobs):
            P.tag = jobs[i][2]
            if parts is not None:
                n = pos[i]
                while try_issue(n):
                    pass
                assert i in views, "ring too small for chunk"
                fn(*views.pop(i))
                allocs.pop(n, None)
            else:
                fn()

    def wview(dram, l, r0, nrows, c0, ncol):
        kc = nrows // 128
        return (dram[l, r0:r0 + nrows, c0:c0 + ncol].rearrange("(k p) n -> p k n", p=128), kc, ncol)

    for c in range(KC):
        dma("sp", xT[:, c, :], xT_d[c * 128:(c + 1) * 128, :], [], [xB[c][t] for t in range(5)], xload)
    dma("sp", vecs[:], vecs_d[:, :], [], [bf("vecs")], misc_ld)
    dma("sp", cT.rearrange("p c j -> p (c j)"), cT_d[:, :], [], [bf("cT")], misc_ld)
    dma("pool", constb[:], consts_d[:, :], [], [bf("constb")], bf("constb"))
    P.add("dve", lambda e: e.memset(ones[:], 1.0), [], [bf("ones")])
    act(scT[:].rearrange("p c j -> p (c j)"), cT.rearrange("p c j -> p (c j)"), AF.Silu, [bf("cT")], [bf("scT")])
    P.barrier()

    identb = constb[:, C_ID:C_ID + 128]
    mask2 = constb[:, C_MPREV:C_MPREV + 256]
    mask_own = constb[:, C_MOWN:C_MOWN + 128]

    def rms_to_h_group(l, which, tiles, dst, dstB, rstd_all, nt=None):
        sq_t, tmp_t, _ = nt or normD
        gsv = gs1 if which == 0 else gs2
        off = {}
        o = 0
        for ti in tiles:
            off[ti] = o
            o += TILES[ti][1]
        for ti in tiles:
            t0, n = TILES[ti]
            rs = rstd_all[:, off[ti]:off[ti] + n]
            rB = bf("rstdg%d" % ti)
            bk = nbank()
            for c in range(KC):
                s = c % 2
                act(sq_t[:, s, :n], xT[:, c, t0:t0 + n], AF.Square, [xB[c][ti]], [bf("sq%d" % s)])
                mm(banks[bk][:, :n], ones[:, :], sq_t[:, s, :n], c == 0, c == KC - 1, [bf("sq%d" % s), bf("ones")], bk)
            act(rs, banks[bk][:, :n], AF.Sqrt, [bankB[bk]], [rB], bias=EPS, scale=1.0 / D)
            P.add("dve", lambda e, rs=rs: e.reciprocal(out=rs, in_=rs), [rB], [rB])
        for ti in tiles:
            t0, n = TILES[ti]
            rs = rstd_all[:, off[ti]:off[ti] + n]
            rB = bf("rstdg%d" % ti)
            for c in range(KC):
                s = c % 2
                tt("dve" if ti < 4 else "pool", tmp_t[:, s, :n], xT[:, c, t0:t0 + n], rs, ALU.mult,
                   [xB[c][ti], rB], [bf("tmp%d" % s)])
                if ti < 4:
                    act(dst(c, t0, n), tmp_t[:, s, :n], AF.Identity, [bf("tmp%d" % s), bf("mod")], [dstB(c, ti)],
                        bias=shT[:, which, c, 0:1], scale=gsv[:, c, 0:1])
                else:
                    t3 = tmp_t[:, s, :n].rearrange("p (b t) -> p b t", t=4)
                    tt("dve", t3, t3, gsv[:, c, 1:17].unsqueeze(2).broadcast_to([128, 16, 4]), ALU.mult,
                       [bf("tmp%d" % s), bf("mod")], [bf("tmp%d" % s)])
                    tt("pool", dst(c, t0, n).rearrange("p (b t) -> p b t", t=4), t3,
                       shT[:, which, c, 1:17].unsqueeze(2).broadcast_to([128, 16, 4]), ALU.add,
                       [bf("tmp%d" % s), bf("mod")], [dstB(c, ti)])

    on_state = {}

    def out_norm_chunk(l, which, ti, mp, bk, ytile, ytB, nt=None):
        sq_t, tmp_t, rstd_t = nt or normD
        t0, n = TILES[ti]
        if mp == 0:
            on_state["bank"] = nbank(pin=True)
        sb_ = on_state["bank"]
        s = mp % 2
        act(sq_t[:, s, :n], banks[bk][:, :n], AF.Square, [bankB[bk]], [bf("sq%d" % s)])
        cp("dve", ytile[:, mp, :n], banks[bk][:, :n], [bankB[bk]], [ytB[mp]])
        if mp >= 1:
            s1 = (mp - 1) % 2
            mm(banks[sb_][:, :n], ones[:, :], sq_t[:, s1, :n], mp == 1, False, [bf("sq%d" % s1), bf("ones")], sb_)
        if mp < KC - 1:
            return
        mm(banks[sb_][:, :n], ones[:, :], sq_t[:, s, :n], False, True, [bf("sq%d" % s), bf("ones")], sb_)
        ggv = gg1 if which == 0 else gg2
        act(rstd_t[:, :n], banks[sb_][:, :n], AF.Sqrt, [bankB[sb_]], [bf("rstd")], bias=EPS, scale=1.0 / D)
        unpin(sb_)
        P.add("dve", lambda e: e.reciprocal(out=rstd_t[:, :n], in_=rstd_t[:, :n]), [bf("rstd")], [bf("rstd")])
        for c in range(KC):
            s = c % 2
            tt("pool" if c % 2 == 0 else "dve", tmp_t[:, s, :n], ytile[:, c, :n], rstd_t[:, :n], ALU.mult,
               [ytB[c], bf("rstd")], [bf("tmp%d" % s)])
            if ti < 4:
                stt(xT[:, c, t0:t0 + n], tmp_t[:, s, :n], ggv[:, c, 0:1], xT[:, c, t0:t0 + n], ALU.mult, ALU.add,
                    [bf("tmp%d" % s), bf("mod"), xB[c][ti]], [xB[c][ti]])
            else:
                t3 = tmp_t[:, s, :n].rearrange("p (b t) -> p b t", t=4)
                tt("dve", t3, t3, ggv[:, c, 1:17].unsqueeze(2).broadcast_to([128, 16, 4]), ALU.mult,
                   [bf("tmp%d" % s), bf("mod")], [bf("tmp%d" % s)])
                tt("dve", xT[:, c, t0:t0 + n], xT[:, c, t0:t0 + n], tmp_t[:, s, :n], ALU.add,
                   [bf("tmp%d" % s), xB[c][ti]], [xB[c][ti]])

    def layer(l):
        vb = lambda off, n: vecs[:, off:off + n]

        settag('M')
        def m_chunk(lm, j):
            def fn(ws, wb):
                w = ws[0]
                mb = mstate.setdefault(lm, {})
                for m in range(2):
                    mi = j * 2 + m
                    if mi % 24 == 0:
                        mb[mi // 24] = nbank(pin=True)
                    bk = mb[mi // 24]
                    col = (mi % 24) * 17
                    for kc in range(KC):
                        mm(banks[bk][:, col:col + 17], w[:, kc, m * 128:(m + 1) * 128], scT[:, kc, :],
                           kc == 0, kc == KC - 1, [wb, bf("scT")], bk)
            return fn

        def m_job(lm, j):
            job([wview(ada_w_d, lm, 0, D, j * 256, 256)], m_chunk(lm, j))

        if l == 0:
            for j in range(24):
                m_job(0, j)

        def m_evac():
            for h in range(2):
                bk = mstate[l][h]
                tt("dve", modT[:, h * 24:(h + 1) * 24, :],
                   banks[bk][:, 0:408].rearrange("p (m j) -> p m j", j=17),
                   vecs[:, V_ADAB + l * 48 + h * 24:V_ADAB + l * 48 + (h + 1) * 24].unsqueeze(2).broadcast_to([128, 24, 17]),
                   ALU.add, [bankB[bk], bf("vecs")], [bf("mod")])
                unpin(bk)

        job(None, m_evac)

        def m_derive():
            ng = lambda k: vecs[:, V_NORMG + l * 32 + k * 8:V_NORMG + l * 32 + (k + 1) * 8].unsqueeze(2).broadcast_to([128, 8, 17])
            ts("dve", gs1[:], modT[:, 8:16, :], 1.0, None, ALU.add, None, [bf("mod")], [bf("mod")])
            tt("dve", gs1[:], gs1[:], ng(0), ALU.mult, [bf("mod"), bf("vecs")], [bf("mod")])
            tt("dve", gg1[:], modT[:, 16:24, :], ng(1), ALU.mult, [bf("mod"), bf("vecs")], [bf("mod")])
            ts("dve", gs2[:], modT[:, 32:40, :], 1.0, None, ALU.add, None, [bf("mod")], [bf("mod")])
            tt("dve", gs2[:], gs2[:], ng(2), ALU.mult, [bf("mod"), bf("vecs")], [bf("mod")])
            tt("dve", gg2[:], modT[:, 40:48, :], ng(3), ALU.mult, [bf("mod"), bf("vecs")], [bf("mod")])
            cp("dve", shT[:, 0, :, :], modT[:, 0:8, :], [bf("mod")], [bf("mod")])
            cp("dve", shT[:, 1, :, :], modT[:, 24:32, :], [bf("mod")], [bf("mod")])
            dma("sp", lnv, lnv_d[l], [], [bf("lnv")], bf("lnv"))
            dma("pool", wspb, wspT_d[l], [], [bf("wspb")], bf("wspb"))
            dma("pool", wssb, wsps_d[l], [], [bf("wssb")], bf("wssb"))
            P.add("dve", lambda e: e.memset(smallreg[:, 5120:5888], 0.0), [], [bf("qblk")])
            tt("dve", wspb.rearrange("p (g t) -> p g t", g=4), wspb.rearrange("p (g t) -> p g t", g=4),
               mask_own.unsqueeze(1).broadcast_to([128, 4, 128]), ALU.mult, [bf("wspb"), bf("constb")], [bf("wspb")])
            tt("dve", wssb.rearrange("p (g t) -> p g t", g=4), wssb.rearrange("p (g t) -> p g t", g=4),
               constb[0:64, C_SSP:C_SSP + 64].unsqueeze(1).broadcast_to([64, 4, 64]), ALU.mult,
               [bf("wssb"), bf("constb")], [bf("wssb")])

        job(None, m_derive)

        settag('N1')
        def n1():
            rms_to_h_group(l, 0, list(range(5)), lambda c, t0, n: hTv[:, c, t0:t0 + n], lambda c, ti_: hB[c][ti_],
                           arena2[:, 2048:2048 + 2 * NT].bitcast(F32))

        job(None, n1)

        settag('A')
        def phase_a1(ws, wb):
            wu = ws[0]
            for ti in range(5):
                t0, n = TILES[ti]
                yab = [yaB[ti * 4 + s_] for s_ in range(4)] if ti < 4 else [yaB[16]]
                for cg in range(4):
                    bk = nbank()
                    for kc in range(KC):
                        mm(banks[bk][:, :n], wu[:, kc, cg * 128:(cg + 1) * 128], hTv[:, kc, t0:t0 + n],
                           kc == 0, kc == KC - 1, [wb, hB[kc][ti]], bk)
                    act(yaT[:, cg, t0:t0 + n], banks[bk][:, :n], AF.Gelu_apprx_tanh, [bankB[bk]], yab)

        def phase_a2(ws, wb):
            wv = ws[0]
            chunks = []
            for ti in range(5):
                for sj in range(4 if ti < 4 else 1):
                    chunks.append((ti, sj))
            vgs = [vg_f, arena2[:, A2T:A2T + 1024].bitcast(F32)]
            sts = [small, arena2[:, A2T + 1024:A2T + 1056].bitcast(F32)]

            def stage1(k):
                ti, sj = chunks[k]
                t0, n = TILES[ti]
                par = k % 2
                vg, st = vgs[par], sts[par]
                pn = 128 if ti < 4 else 64
                c0 = t0 + sj * 128
                bk = nbank()
                for kc in range(KC):
                    mm(banks[bk][:pn, :], hTv[:, kc, c0:c0 + pn], wv[:, kc, :], kc == 0, kc == KC - 1,
                       [wb, hB[kc][ti]], bk)
                act(vg[:pn, :], banks[bk][:pn, :], AF.Gelu_apprx_tanh, [bankB[bk]], [bf("vg%d" % par)])
                P.add("dve", lambda e: e.bn_stats(out=st[:pn, 0:6], in_=vg[:pn, :]), [bf("vg%d" % par)], [bf("bnst%d" % par)])
                P.add("dve", lambda e: e.bn_aggr(out=st[:pn, 6:8], in_=st[:pn, 0:6]), [bf("bnst%d" % par)], [bf("bnag%d" % par)])
                act(st[:pn, 8:9], st[:pn, 7:8], AF.Sqrt, [bf("bnag%d" % par)], [bf("lnr%d" % par)], bias=EPS, scale=1.0)
                P.add("dve", lambda e: e.reciprocal(out=st[:pn, 9:10], in_=st[:pn, 8:9]), [bf("lnr%d" % par)], [bf("lnr2%d" % par)])

            def stage2(k):
                ti, sj = chunks[k]
                t0, n = TILES[ti]
                par = k % 2
                vg, st = vgs[par], sts[par]
                j = ti * 4 + sj
                pn = 128 if ti < 4 else 64
                c0 = t0 + sj * 128
                ts("dve", z_f[:pn, :], vg[:pn, :], st[:pn, 6:7], st[:pn, 9:10], ALU.subtract, ALU.mult,
                   [bf("vg%d" % par), bf("bnag%d" % par), bf("lnr2%d" % par)], [bf("zf")])
                tt("pool", z_f[:pn, :], z_f[:pn, :], lnv[:pn, 0:512], ALU.mult, [bf("zf"), bf("lnv")], [bf("zf")])
                if ti < 4:
                    tt("pool", vn_b[:pn, :], z_f[:pn, :], lnv[:pn, 512:1024], ALU.add, [bf("zf"), bf("lnv")], [bf("vnb")])
                else:
                    tt("pool", vn_f[:pn, :], z_f[:pn, :], lnv[:pn, 512:1024], ALU.add, [bf("zf"), bf("lnv")], [bf("vnf")])
                    cp("dve", vn_b[:pn, :], vn_f[:pn, :], [bf("vnf")], [bf("vnb")])
                    dma("sp", vns_d[l], vn_f[:pn, :], [bf("vnf")], [], bf("vnf"))
                    if bf("vnf") not in out_keys:
                        out_keys.append(bf("vnf"))
                bk2 = nbank()
                for g in range(4):
                    if ti < 4:
                        o = banks[bk2][:, g * 128:(g + 1) * 128]
                        mm(o, vn_b[:, g * 128:(g + 1) * 128], wspb[:, g * 128:(g + 1) * 128], True, True,
                           [bf("vnb"), bf("wspb")], bk2)
                    else:
                        o = banks[bk2][:, g * 64:(g + 1) * 64]
                        mm(o, vn_b[0:64, g * 128:(g + 1) * 128], wssb[0:64, g * 64:(g + 1) * 64], True, True,
                           [bf("vnb"), bf("wssb")], bk2)
                boff = 1024 if ti < 4 else 1536
                tt("dve", z_f[:, 0:4 * pn], banks[bk2][:, 0:4 * pn], lnv[:, boff:boff + 4 * pn], ALU.add,
                   [bankB[bk2], bf("lnv")], [bf("zf")])
                tt("dve", yaT[:, :, c0:c0 + pn], yaT[:, :, c0:c0 + pn],
                   z_f[:, 0:4 * pn].rearrange("p (g t) -> p g t", g=4), ALU.mult,
                   [yaB[j], bf("zf")], [yaB[j]])

            stage1(0)
            for k in range(len(chunks)):
                if k + 1 < len(chunks):
                    stage1(k + 1)
                stage2(k)

        job(None, P.barrier)
        job([wview(w_in_d, l, 0, D, 0, 512)], phase_a1)
        job([wview(w_in_d, l, 0, D, 512, 512)], phase_a2)
        job(None, P.barrier)

        settag('B')
        def phase_b(pair, g):
            dil = DILS[g]
            nsub = 16 // dil
            sidx = g * 2 + pair

            def fn(ws, wb):
                w = ws.whole
                for which in range(2):
                    dstT = qT if which == 0 else kT
                    for ti in range(5):
                        t0, n = TILES[ti]
                        bk = nbank()
                        for kc in range(KC):
                            mm(banks[bk][:, :n], w[:, kc, which * 128:(which + 1) * 128], hTv[:, kc, t0:t0 + n],
                               kc == 0, kc == KC - 1, [wb, hB[kc][ti]], bk)
                        if ti < 4:
                            src = banks[bk][:, 0:512].rearrange("p (n r) -> p r n", r=dil)
                            dst = dstT.rearrange("p (r n) -> p r n", r=dil)[:, :, ti * (512 // dil):(ti + 1) * (512 // dil)]
                            cp("act" if which == 0 else "dve", dst, src, [bankB[bk]], [bf("qT" if which == 0 else "kT")])
                        elif which == 0:
                            cp("act", qsT[:, sidx, :], banks[bk][:, 0:64], [bankB[bk]], [bf("qsT")])
                            cp("dve", qblk[0:64, sidx, :, 0:4], banks[bk][0:64, 0:64].rearrange("p (b t) -> p b t", t=4),
                               [bankB[bk]], [bf("qblk")])
                            cp("dve", qblk[64:128, sidx, :, 4:8], banks[bk][64:128, 0:64].rearrange("p (b t) -> p b t", t=4),
                               [bankB[bk]], [bf("qblk")])
                        else:
                            cp("act", ksT[:, sidx, :], banks[bk][:, 0:64], [bankB[bk]], [bf("ksT")])
                keep_from = {0: NP_ - 128, 1: NP_ - 512, 2: 0}[g]
                for blk in range(17):
                    bk = nbank()
                    if blk < 16:
                        r, nb = blk // nsub, blk % nsub
                        start = r + dil * 128 * nb
                        pn = 128
                        if start >= keep_from:
                            for kc in range(KC):
                                mm(banks[bk][:, 0:256], hTv[:, kc, start:start + dil * 127 + 1:dil], w[:, kc, 128:384],
                                   kc == 0, kc == KC - 1, [wb] + [hB[kc][t] for t in range(4)], bk)
                        else:
                            for kc in range(KC):
                                mm(banks[bk][:, 128:256], hTv[:, kc, start:start + dil * 127 + 1:dil], w[:, kc, 256:384],
                                   kc == 0, kc == KC - 1, [wb] + [hB[kc][t] for t in range(4)], bk)
                    else:
                        pn = 64
                        for kc in range(KC):
                            mm(banks[bk][:64, 0:256], hTv[:, kc, NP_:NT], w[:, kc, 128:384],
                               kc == 0, kc == KC - 1, [wb, hB[kc][4]], bk)
                    if blk < 16:
                        s = blk % 2
                        kept = start >= keep_from
                        if kept:
                            cp("act", kvst[s][:, :], banks[bk][:, 0:256], [bankB[bk]], [bf("kvst%d" % s)])
                            dst = kvp_d[g][l, start - keep_from:start - keep_from + dil * 127 + 1:dil, :]
                            dst = dst.rearrange("t (kv c) -> t kv c", kv=2)[:, :, pair * 128:(pair + 1) * 128]
                            dma("sp", dst, kvst[s][:, :].rearrange("p (kv c) -> p kv c", kv=2), [bf("kvst%d" % s)], [],
                                bf("kvst%d" % s))
                            if bf("kvst%d" % s) not in out_keys:
                                out_keys.append(bf("kvst%d" % s))
                        cp("dve", vtok[:, blk, :], banks[bk][:, 128:256], [bankB[bk]], [bf("vtok")])
                    else:
                        cp("act", kvs_st[:, :], banks[bk][:64, 0:256], [bankB[bk]], [bf("kvs_st")])
                        dma("sp", kvs_d[l, g].rearrange("t (kv c) -> t kv c", kv=2)[:, :, pair * 128:(pair + 1) * 128],
                            kvs_st[:, :].rearrange("p (kv c) -> p kv c", kv=2), [bf("kvs_st")], [], bf("kvs_st"))
                        if bf("kvs_st") not in out_keys:
                            out_keys.append(bf("kvs_st"))
                        cp("dve", vs_b[:, sidx, :], banks[bk][:64, 128:256], [bankB[bk]], [bf("vs_b")])
                st = {}

                def stage_s(qb):
                    r, nb = qb // nsub, qb % nsub
                    kbs = ([qb - 1] if nb > 0 else []) + [qb]
                    nk = len(kbs)
                    ba, bb = nbank(), nbank()
                    for h2, bk in ((0, ba), (1, bb)):
                        for i, kb in enumerate(kbs):
                            mm(banks[bk][:, i * 128:(i + 1) * 128], kT[h2 * 64:(h2 + 1) * 64, kb * 128:(kb + 1) * 128],
                               qT[h2 * 64:(h2 + 1) * 64, qb * 128:(qb + 1) * 128], True, True, [bf("qT"), bf("kT")], bk)
                    pts = []
                    for h2, bk in ((0, ba), (1, bb)):
                        s = (qb % 2) * 2 + h2
                        act(pex[s][:, :nk * 128], banks[bk][:, :nk * 128], AF.Exp, [bankB[bk]], [bf("pex%d" % s)], scale=0.125)
                        m = mask2 if nk == 2 else mask_own
                        tt("dve", pT[s][:, :nk * 128], pex[s][:, :nk * 128], m, ALU.mult, [bf("pex%d" % s), bf("constb")],
                           [bf("pT%d" % s)])
                        pts.append(s)
                    st[qb] = (kbs, pts)

                def stage_pv(qb):
                    kbs, pts = st.pop(qb)
                    r, nb = qb // nsub, qb % nsub
                    bu, bz = nbank(), nbank()
                    nk = len(kbs)
                    for i, kb in enumerate(kbs):
                        for h2 in range(2):
                            mm(banks[bu][h2 * 64:(h2 + 1) * 64, 0:128], vtok[:, kb, h2 * 64:(h2 + 1) * 64],
                               pT[pts[h2]][:, i * 128:(i + 1) * 128], i == 0, i == nk - 1, [bf("vtok"), bf("pT%d" % pts[h2])], bu)
                    for i, kb in enumerate(kbs):
                        for h2 in range(2):
                            mm(banks[bz][h2 * 64:(h2 + 1) * 64, 0:128], ones[:, 0:64],
                               pT[pts[h2]][:, i * 128:(i + 1) * 128], i == 0, i == nk - 1, [bf("ones"), bf("pT%d" % pts[h2])], bz)
                    start = r + dil * 128 * nb
                    sl = slice(start, start + dil * 127 + 1, dil)
                    if g == 0:
                        cp("act", uacc[:, sl], banks[bu][:, 0:128], [bankB[bu]], [bf("uacc")])
                        cp("dve", zacc[:, sl], banks[bz][:, 0:128], [bankB[bz]], [bf("zacc")])
                    else:
                        tt("dve", uacc[:, sl], uacc[:, sl], banks[bu][:, 0:128], ALU.add, [bf("uacc"), bankB[bu]], [bf("uacc")])
                        tt("dve", zacc[:, sl], zacc[:, sl], banks[bz][:, 0:128], ALU.add, [bf("zacc"), bankB[bz]], [bf("zacc")])

                for qb in range(17):
                    if qb < 16:
                        stage_s(qb)
                    if qb >= 1:
                        stage_pv(qb - 1)
                if g == 2:
                    P.add("dve", lambda e: e.reciprocal(out=zacc[:, :], in_=zacc[:, :]), [bf("zacc")], [bf("zacc")])
                    tt("dve", ybT[:, pair, 0:NP_], uacc[:, :], zacc[:, :], ALU.mult, [bf("uacc"), bf("zacc")], [ybB[pair][0]])
            return fn

        for pair in range(2):
            for g in range(3):
                cq = O0 + g * 256 + pair * 128
                job([wview(w_in_d, l, 0, D, cq, 128), wview(w_in_d, l, 0, D, cq + 768, 128),
                     wview(w_in_d, l, 0, D, cq + 1536, 128)], phase_b(pair, g))

        settag('BS')
        def phase_bs():
            bU, bZ = nbank(pin=True), nbank(pin=True)
            first = {0: True, 1: True}
            for g in range(3):
                be, bo = nbank(), nbank()
                for pair in range(2):
                    sidx = g * 2 + pair
                    for h2, bk in ((0, be), (1, bo)):
                        mm(banks[bk][0:64, pair * 64:(pair + 1) * 64], ksT[h2 * 64:(h2 + 1) * 64, sidx, :],
                           qsT[h2 * 64:(h2 + 1) * 64, sidx, :], True, True, [bf("ksT"), bf("qsT")], bk)
                msk = constb[0:64, C_SNEW0:C_SNEW0 + 64] if g == 0 else constb[0:64, C_SNEW1:C_SNEW1 + 64]
                for h2, bk in ((0, be), (1, bo)):
                    act(pnx[0:64, h2 * 128:(h2 + 1) * 128], banks[bk][0:64, 0:128], AF.Exp, [bankB[bk]], [bf("pnx")], scale=0.125)
                for h2 in range(2):
                    tt("dve", pnew[0:64, g, h2 * 2:(h2 + 1) * 2, :],
                       pnx[0:64, h2 * 128:(h2 + 1) * 128].rearrange("p (a q) -> p a q", a=2),
                       msk.unsqueeze(1).broadcast_to([64, 2, 64]), ALU.mult, [bf("pnx"), bf("constb")], [bf("pnew")])
                for pair in range(2):
                    sidx = g * 2 + pair
                    for h2 in range(2):
                        rhs = pnew[0:64, g, h2 * 2 + pair, :]
                        mm(banks[bU][h2 * 64:(h2 + 1) * 64, pair * 64:(pair + 1) * 64], vs_b[0:64, sidx, h2 * 64:(h2 + 1) * 64],
                           rhs, first[h2], False, [bf("vs_b"), bf("pnew")], bU, skip=True)
                        mm(banks[bZ][h2 * 64:(h2 + 1) * 64, pair * 64:(pair + 1) * 64], ones[0:64, 0:64],
                           rhs, first[h2], False, [bf("ones"), bf("pnew")], bZ, skip=True)
                        first[h2] = False
            units = [(g, b) for g in range(3) for b in range(16)]
            stt_ = {}

            def s_load(u):
                g, b = units[u]
                s = u % NKV
                nt = 1 if g == 0 else 4
                if g == 0:
                    src = cache_d[0][l, b].rearrange("(t i) c -> i t c", t=1)
                elif g == 1:
                    src = cache_d[1][l, b].rearrange("(i r) c -> i r c", r=4)
                else:
                    src = cache_d[2][l, b].rearrange("(i r) c -> i r c", r=16)[:, 0:4, :]
                dma("pool", kvc[s][:, 0:nt, :], src, [], [bf("kvc%d" % s)], bf("kvc%d" % s))

            def s_tr(u):
                g, b = units[u]
                s = u % NKV
                s2 = u % 2
                nt = 1 if g == 0 else 4
                bk = nbank()
                pv = banks[bk][:].bitcast(BF16)
                for t in range(nt):
                    for pair in range(2):
                        i = t * 2 + pair
                        P.add("pe", lambda e, o=pv[:, i * 128:(i + 1) * 128], a=kvc[s][:, t, pair * 128:(pair + 1) * 128]:
                              e.transpose(out=o, in_=a, identity=identb), [bf("kvc%d" % s), bf("constb")], [bankB[bk]])
                cp("act", kcT[s2][:, 0:nt * 256], pv[:, 0:nt * 256], [bankB[bk]], [bf("kcT%d" % s2)])

            def s_sc(u):
                g, b = units[u]
                s2 = u % 2
                nt = 1 if g == 0 else 4
                bk = nbank()
                for t in range(nt):
                    for pair in range(2):
                        i = t * 2 + pair
                        mm(banks[bk][:, i * 8:(i + 1) * 8], kcT[s2][:, i * 128:(i + 1) * 128], qblk[:, g * 2 + pair, b, :],
                           True, True, [bf("kcT%d" % s2), bf("qblk")], bk)
                nc_ = nt * 16
                act(pcs[s2][:, 0:nc_], banks[bk][:, 0:nc_], AF.Exp, [bankB[bk]], [bf("pcs%d" % s2)], scale=0.125)
                msk = constb[:, C_MC0:C_MC0 + 16] if g == 0 else constb[:, C_MC1:C_MC1 + 64]
                tt("dve", pcm[s2][:, 0:nc_], pcs[s2][:, 0:nc_], msk, ALU.mult, [bf("pcs%d" % s2), bf("constb")], [bf("pcm%d" % s2)])

            def s_pv(u):
                g, b = units[u]
                s = u % NKV
                s2 = u % 2
                nt = 1 if g == 0 else 4
                for t in range(nt):
                    for pair in range(2):
                        for h2 in range(2):
                            col = (t * 2 + pair) * 8 + h2 * 4
                            rhs = pcm[s2][:, col:col + 4]
                            o = slice(pair * 64 + 4 * b, pair * 64 + 4 * b + 4)
                            mm(banks[bU][h2 * 64:(h2 + 1) * 64, o], kvc[s][:, t, 256 + (pair * 2 + h2) * 64:256 + (pair * 2 + h2 + 1) * 64],
                               rhs, False, False, [bf("kvc%d" % s), bf("pcm%d" % s2)], bU, skip=True)
                            mm(banks[bZ][h2 * 64:(h2 + 1) * 64, o], ones[:, 0:64],
                               rhs, False, False, [bf("ones"), bf("pcm%d" % s2)], bZ, skip=True)

            nu = len(units)
            s_load(0)
            s_load(1)
            for step in range(nu + 3):
                if 3 <= step:
                    s_pv(step - 3)
                if step + 2 < nu:
                    s_load(step + 2)
                if 1 <= step < nu + 1:
                    s_tr(step - 1)
                if 2 <= step < nu + 2:
                    s_sc(step - 2)
            P.add("dve", lambda e: e.reciprocal(out=rzs[:, :], in_=banks[bZ][:, 0:128]), [bankB[bZ]], [bf("rzs")])
            tt("dve", ybT[:, :, NP_:NT], banks[bU][:, 0:128].rearrange("p (a q) -> p a q", a=2),
               rzs[:, :].rearrange("p (a q) -> p a q", a=2), ALU.mult, [bankB[bU], bf("rzs")], [ybB[0][1], ybB[1][1]])
            unpin(bU)
            unpin(bZ)

        job(None, P.barrier)
        job(None, phase_bs)
        job(None, P.barrier)

        settag('C')
        for gi, grp in enumerate(GROUPS):
            g0t = TILES[grp[0]][0]

            def c_chunk(m, grp=grp, g0t=g0t):
                def fn(ws, wb):
                    wga, wgb, wa, wbb = ws
                    for ti in grp:
                        t0, n = TILES[ti]
                        b1, b2, b3, b4 = nbank(), nbank(), nbank(), nbank()
                        for kc in range(KC):
                            mm(banks[b1][:, :n], wga[:, kc, :], hTv[:, kc, t0:t0 + n], kc == 0, kc == KC - 1, [wb, hB[kc][ti]], b1)
                        for kc in range(KC):
                            mm(banks[b2][:, :n], wgb[:, kc, :], hTv[:, kc, t0:t0 + n], kc == 0, kc == KC - 1, [wb, hB[kc][ti]], b2)
                        yab = [yaB[ti * 4 + s] for s in range(4)] if ti < 4 else [yaB[16]]
                        for kc in range(4):
                            mm(banks[b3][:, :n], wa[:, kc, :], yaT[:, kc, t0:t0 + n], kc == 0, kc == 3, [wb] + yab, b3)
                        ybb = [ybB[0][0], ybB[1][0]] if ti < 4 else [ybB[0][1], ybB[1][1]]
                        for kc in range(2):
                            mm(banks[b4][:, :n], wbb[:, kc, :], ybT[:, kc, t0:t0 + n], kc == 0, kc == 1, [wb] + ybb, b4)
                        act(ga_t[:, :n], banks[b1][:, :n], AF.Sigmoid, [bankB[b1]], [bf("ga")])
                        act(gb_t[:, :n], banks[b2][:, :n], AF.Sigmoid, [bankB[b2]], [bf("gb")])
                        tt("dve", t1_t[:, :n], ga_t[:, :n], banks[b3][:, :n], ALU.mult, [bf("ga"), bankB[b3]], [bf("t1")])
                        tt("dve", t2_t[:, :n], gb_t[:, :n], banks[b4][:, :n], ALU.mult, [bf("gb"), bankB[b4]], [bf("t2")])
                        tt("pool", merged_c(m)[:, t0 - g0t:t0 - g0t + n], t1_t[:, :n], t2_t[:, :n], ALU.add,
                           [bf("t1"), bf("t2")], [bf("mg%d_%d" % (m, ti))])
                return fn

            for m in range(KC):
                job([wview(w_in_d, l, 0, D, O1 + m * 128, 128), wview(w_in_d, l, 0, D, O1 + D + m * 128, 128),
                     wview(w_a2d_d, l, 0, 512, m * 128, 128), wview(w_b2d_d, l, 0, 256, m * 128, 128)], c_chunk(m))

            job(None, P.barrier)
            for ti in grp:
                def o_chunk(mp, ti=ti, g0t=g0t):
                    def fn(ws, wb):
                        w = ws[0]
                        t0, n = TILES[ti]
                        bk = nbank()
                        for m in range(KC):
                            mm(banks[bk][:, :n], w[:, m, :], merged_c(m)[:, t0 - g0t:t0 - g0t + n], m == 0, m == KC - 1,
                               [wb, bf("mg%d_%d" % (m, ti))], bk)
                        out_norm_chunk(l, 0, ti, mp, bk, ytile_c, [bf("yt%d" % c) for c in range(KC)], normC)
                    return fn
                for mp in range(KC):
                    job([wview(w_out_d, l, 0, D, mp * 128, 128)], o_chunk(mp))
            job(None, P.barrier)

        settag('D')
        job(None, P.barrier)
        cw = lambda c, j: vecs[:, V_CONVW + l * 66 + c * 3 + j:V_CONVW + l * 66 + c * 3 + j + 1]
        cb = lambda c: vecs[:, V_CONVB + l * 22 + c:V_CONVB + l * 22 + c + 1]
        for gi, grp in enumerate(GROUPS):
            g0t = TILES[grp[0]][0]

            def n2(grp=grp, g0t=g0t):
                rms_to_h_group(l, 1, grp, lambda c, t0, n: h2T[:, c, t0 - g0t:t0 - g0t + n],
                               lambda c, ti_: bf("h2_%d_%d" % (c, ti_)), arena2[:, 4 * GB:4 * GB + 2 * GMAX].bitcast(F32))

            job(None, n2)
            job(None, P.barrier)

            def d_chunk(c, grp=grp, g0t=g0t, gi=gi):
                def fn(ws, wb):
                    wg, wv = ws
                    s = c % 2
                    gb_ = gbuf[s]
                    gB = bf("gbuf%d" % s)
                    if gi == 0:
                        P.add("dve", lambda e: e.memset(gb_[:, 0:2], 0.0), [], [gB])
                    else:
                        cp("dve", gb_[:, 0:2], halo[:, c, :], [bf("halo")], [gB])
                    for ti in grp:
                        t0, n = TILES[ti]
                        lo = t0 - g0t
                        b1, b2 = nbank(), nbank()
                        for kc in range(KC):
                            mm(banks[b1][:, :n], wg[:, kc, :], h2T[:, kc, lo:lo + n], kc == 0, kc == KC - 1,
                               [wb, bf("h2_%d_%d" % (kc, ti))], b1)
                        for kc in range(KC):
                            mm(banks[b2][:, :n], wv[:, kc, :], h2T[:, kc, lo:lo + n], kc == 0, kc == KC - 1,
                               [wb, bf("h2_%d_%d" % (kc, ti))], b2)
                        a_ = acc_t[s]
                        aB = bf("acc%d" % s)
                        if ti < 4:
                            cp("act", gb_[:, 2 + lo:2 + lo + n], banks[b1][:, :n], [bankB[b1]], [gB])
                            act(a_[:, :n], banks[b1][:, :n], AF.Identity, [bankB[b1], bf("vecs")], [aB], bias=cb(c), scale=cw(c, 2))
                            stt(a_[:, :n], gb_[:, 1 + lo:1 + lo + n], cw(c, 1), a_[:, :n], ALU.mult, ALU.add, [gB, aB, bf("vecs")], [aB])
                            stt(a_[:, :n], gb_[:, lo:lo + n], cw(c, 0), a_[:, :n], ALU.mult, ALU.add, [gB, aB, bf("vecs")], [aB])
                            if ti == 3:
                                cp("act", convp_st[:, c, :], gb_[:, 2 + lo + n - 2:2 + lo + n], [gB], [bf("convp_st")])
                            if gi < len(GROUPS) - 1 and ti == grp[-1]:
                                cp("act", halo[:, c, :], gb_[:, 2 + lo + n - 2:2 + lo + n], [gB], [bf("halo")])
                        else:
                            g6 = gbs[s]
                            g6B = bf("gbs%d" % s)
                            if g6B not in out_keys:
                                out_keys.append(g6B)
                            dma("sp", g6[:, 0:2, :], convst_d[l][:, c * 32:(c + 1) * 32].rearrange("p (j b) -> p j b", j=2),
                                [], [g6B], g6B)
                            pg = banks[b1][:, 0:64].rearrange("p (b t) -> p t b", t=4)
                            cp("act", g6[:, 2:6, :], pg, [bankB[b1]], [g6B])
                            a3 = a_[:, 0:64].rearrange("p (t b) -> p t b", t=4)
                            act(a3, pg, AF.Identity, [bankB[b1], bf("vecs")], [aB], bias=cb(c), scale=cw(c, 2))
                            stt(a3, g6[:, 1:5, :], cw(c, 1), a3, ALU.mult, ALU.add, [g6B, aB, bf("vecs")], [aB])
                            stt(a3, g6[:, 0:4, :], cw(c, 0), a3, ALU.mult, ALU.add, [g6B, aB, bf("vecs")], [aB])
                            dma("sp", convs_d[l][:, c * 32:(c + 1) * 32].rearrange("p (j b) -> p j b", j=2), g6[:, 4:6, :],
                                [g6B], [], g6B)
                            act(ge_t[s][:, 0:64], a_[:, 0:64], AF.Gelu_apprx_tanh, [aB], [bf("ge%d" % s)])
                            tt("dve", act_c(c)[:, lo:lo + 64].rearrange("p (b t) -> p t b", t=4),
                               ge_t[s][:, 0:64].rearrange("p (t b) -> p t b", t=4),
                               banks[b2][:, 0:64].rearrange("p (b t) -> p t b", t=4), ALU.mult,
                               [bf("ge%d" % s), bankB[b2]], [bf("act%d_%d" % (c, ti))])
                            continue
                        act(ge_t[s][:, :n], a_[:, :n], AF.Gelu_apprx_tanh, [aB], [bf("ge%d" % s)])
                        tt("dve", act_c(c)[:, lo:lo + n], ge_t[s][:, :n], banks[b2][:, :n], ALU.mult,
                           [bf("ge%d" % s), bankB[b2]], [bf("act%d_%d" % (c, ti))])
                return fn

            for c in range(FC):
                job([wview(w_up_d, l, 0, D, c * 128, 128), wview(w_up_d, l, 0, D, DFF + c * 128, 128)], d_chunk(c))
                if gi == len(GROUPS) - 1 and l + 1 < depth:
                    m_job(l + 1, c)
            if gi == len(GROUPS) - 1 and l + 1 < depth:
                m_job(l + 1, 22)
                m_job(l + 1, 23)

            for ti in grp:
                def dn_chunk(mp, ti=ti, g0t=g0t):
                    def fn(ws, wb):
                        w = ws[0]
                        t0, n = TILES[ti]
                        lo = t0 - g0t
                        bk = nbank()
                        for c in range(FC):
                            mm(banks[bk][:, :n], w[:, c, :], act_c(c)[:, lo:lo + n], c == 0, c == FC - 1,
                               [wb, bf("act%d_%d" % (c, ti))], bk)
                        out_norm_chunk(l, 1, ti, mp, bk, ytile_d, [bf("ytd%d" % c) for c in range(KC)])
                    return fn
                for mp in range(KC):
                    job([wview(w_down_d, l, 0, DFF, mp * 128, 128)], dn_chunk(mp))

        def l_out():
            dma("sp", convp_d[l], convp_st[:].rearrange("p c j -> p (c j)"), [bf("convp_st")], [], bf("convp_st"))
            if bf("convp_st") not in out_keys:
                out_keys.append(bf("convp_st"))

        job(None, l_out)
        job(None, P.barrier)

    for l in range(depth):
        layer(l)

    def final_out():
        for c in range(KC):
            dma("sp", yT_d[c * 128:(c + 1) * 128, :], xT[:, c, :], [xB[c][t] for t in range(5)], [], ystore)
        out_keys.append(ystore)

    if stop is not None:
        order = ['start', 'M', 'N1', 'A', 'B', 'BS', 'C', 'D']
        keep = set(order[:order.index(stop) + 1])
        jobs[:] = [j for j in jobs if j[2] in keep]
    settag('final')
    job(None, final_out)
    run_jobs()

    P.finalize(nc, stack)
    finals = [(k.sem, 16 * k.cnt) for k in out_keys if k.sem is not None]
    with nc.Block() as block:
        @block.tensor
        def _(e):
            P.emit("pe", e)

        @block.scalar
        def _(e):
            P.emit("act", e)

        @block.vector
        def _(e):
            P.emit("dve", e)

        @block.gpsimd
        def _(e):
            P.emit("pool", e)

        @block.sync
        def _(e):
            P.emit("sp", e, finals)
    stack.close()
    return nc, P


def _consts():
    c = np.zeros((128, NCON), np.float32)
    p = np.arange(128)[:, None]
    q = np.arange(128)[None, :]
    c[:, C_ID:C_ID + 128] = (p == q)
    c[:, C_MPREV:C_MPREV + 128] = (p >= q)
    c[:, C_MOWN:C_MOWN + 128] = (p <= q)
    k = np.arange(64)[:, None]
    qq = np.arange(64)[None, :]
    same = (k // 4) == (qq // 4)
    c[:64, C_SSP:C_SSP + 64] = same & ((k % 4) <= (qq % 4))
    c[:64, C_SNEW0:C_SNEW0 + 64] = same & ((k % 4) <= (qq % 4))
    c[:64, C_SNEW1:C_SNEW1 + 64] = same & ((k % 4) == (qq % 4))
    col = np.arange(16)[None, :]
    c[:, C_MC0:C_MC0 + 16] = (p >= (col % 4))
    col = np.arange(64)[None, :]
    c[:, C_MC1:C_MC1 + 64] = ((col // 16) == (col % 4)) & (p >= 0)
    return c


_PROG = {}


def _get_prog(depth=DEPTH):
    if depth not in _PROG:
        _PROG[depth] = build(depth)[0]
    return _PROG[depth]


def make_in_maps(inputs, cores):
    f = lambda a: np.ascontiguousarray(a, dtype=np.float32)
    ada_b, norm_g = inputs["ada_b"], inputs["norm_g"]
    conv_w, conv_b = inputs["conv_w"], inputs["conv_b"]
    vecs = np.concatenate([
        ada_b.reshape(4, 48, 128).transpose(2, 0, 1).reshape(128, -1),
        norm_g.reshape(4, 4, 8, 128).transpose(3, 0, 1, 2).reshape(128, -1),
        conv_w.reshape(4, 3, 22, 128).transpose(3, 0, 2, 1).reshape(128, -1),
        conv_b.reshape(4, 22, 128).transpose(2, 0, 1).reshape(128, -1)], axis=1)
    assert vecs.shape == (128, NV)
    wsp = inputs["w_spatial"]
    wspT = wsp.transpose(0, 3, 1, 2).reshape(4, 128, 512)
    w4 = wsp[:, :, :4, :4].transpose(0, 3, 1, 2)
    wsps = np.broadcast_to(w4[:, None, :, :, None, :], (4, 16, 4, 4, 16, 4)).reshape(4, 64, 256)
    bs = inputs["b_spatial"]
    lnv = np.concatenate([inputs["ln_v_g"], inputs["ln_v_b"], bs.reshape(4, 512),
                          np.broadcast_to(bs[:, :, None, :4], (4, 4, 16, 4)).reshape(4, 256)], axis=1)
    lnv = np.broadcast_to(lnv[:, None, :], (4, 128, 1792))
    shared = {
        "ada_w": f(inputs["ada_w"]), "w_in": f(inputs["w_in"]), "w_a2d": f(inputs["w_a2d"]),
        "w_b2d": f(inputs["w_b2d"]), "w_out": f(inputs["w_out"]), "w_up": f(inputs["w_up"]),
        "w_down": f(inputs["w_down"]), "vecs": f(vecs), "lnv": f(lnv), "wspT": f(wspT), "wsps": f(wsps),
        "consts": _consts(),
    }
    maps = []
    for i in cores:
        sl = slice(16 * i, 16 * i + 16)
        xs = inputs["x_sample"][sl].reshape(64, D)
        xT = np.concatenate([inputs["x_prompt"][i].T, xs.T], axis=1)
        c = np.concatenate([inputs["c_prompt"][i][None], inputs["c_sample"][sl]], axis=0)
        cT = c.T.reshape(8, 128, 17).transpose(1, 0, 2).reshape(128, 136)
        st = inputs["state_ffn_conv"][:, sl]
        convst = st.reshape(4, 16, 2, 22, 128).transpose(0, 4, 3, 2, 1).reshape(4, 128, FC * 32)
        m = dict(shared)
        m.update({
            "xT": f(xT), "cT": f(cT), "convst": f(convst),
            "cache0": f(inputs["cache_swa0"][:, sl].reshape(4, 16, 128, 512)),
            "cache1": f(inputs["cache_swa1"][:, sl].reshape(4, 16, 512, 512)),
            "cache2": f(inputs["cache_swa2"][:, sl].reshape(4, 16, 2048, 512)),
        })
        maps.append(m)
    return maps


def assemble(results, cores, ncores_total=8):
    nb = ncores_total
    y_p = np.zeros((nb, NP_, D), np.float32)
    y_s = np.zeros((16 * nb, 4, D), np.float32)
    swa_p = [np.zeros((4, nb, k, 2, 4, 64), np.float32) for k in (128, 512, 2048)]
    conv_p = np.zeros((4, nb, 2, DFF), np.float32)
    swa_s = [np.zeros((4, 16 * nb, 4, 2, 4, 64), np.float32) for _ in range(3)]
    conv_s = np.zeros((4, 16 * nb, 2, DFF), np.float32)
    vn_s = np.zeros((4, 16 * nb, 4, 512), np.float32)
    for r, i in zip(results, cores):
        sl = slice(16 * i, 16 * i + 16)
        yT = np.asarray(r["yT"])
        y_p[i] = yT[:, :NP_].T
        y_s[sl] = yT[:, NP_:].T.reshape(16, 4, D)
        for g, k in enumerate((128, 512, 2048)):
            swa_p[g][:, i] = np.asarray(r["kvp%d" % g]).reshape(4, k, 2, 4, 64)
            swa_s[g][:, sl] = np.asarray(r["kvs"])[:, g].reshape(4, 16, 4, 2, 4, 64)
        conv_p[:, i] = np.asarray(r["convp"]).reshape(4, 128, 22, 2).transpose(0, 3, 2, 1).reshape(4, 2, DFF)
        conv_s[:, sl] = np.asarray(r["convs"]).reshape(4, 128, 22, 2, 16).transpose(0, 4, 3, 2, 1).reshape(4, 16, 2, DFF)
        vn_s[:, sl] = np.asarray(r["vns"]).reshape(4, 16, 4, 512)
    return (y_p, y_s, swa_p[0], swa_p[1], swa_p[2], conv_p, swa_s[0], swa_s[1], swa_s[2], conv_s, vn_s)


def kernel(**inputs):
    nc = _get_prog(DEPTH)
    cores = list(range(8))
    maps = make_in_maps(inputs, cores)
    res = run_bass_kernel_spmd(nc, maps, core_ids=cores)
    return assemble(res.results, cores)
```

```python
import numpy as np
from contextlib import ExitStack
import concourse.bass as bass
import concourse.mybir as mybir
from concourse.bass_utils import run_bass_kernel_spmd

F32 = mybir.dt.float32
BF16 = mybir.dt.bfloat16
AF = mybir.ActivationFunctionType
ALU = mybir.AluOpType

D = 1024
KC = 8
NP_ = 2048
NS = 64
NT = NP_ + NS
DEPTH = 4
IN_W = 5376
O0 = 1024
O1 = O0 + 2304
DFF = 2816
FC = 22
EPS = 1e-6
DILS = (1, 4, 16)
TILES = [(0, 512), (512, 512), (1024, 512), (1536, 512), (2048, 64)]
GROUPS = [[0, 1], [2, 3, 4]]
RING_UNITS = 11
RING_UNIT = 1024

V_ADAB = 0
V_NORMG = V_ADAB + 4 * 48
V_CONVW = V_NORMG + 4 * 32
V_CONVB = V_CONVW + 4 * 66
NV = V_CONVB + 4 * 22
C_ID = 0
C_MPREV = 128
C_MOWN = 256
C_SSP = 384
C_SNEW0 = 448
C_SNEW1 = 512
C_MC0 = 576
C_MC1 = 592
NCON = 656


class WS(list):
    whole = None


def _flat(xs):
    out = []
    for x in xs:
        if isinstance(x, (list, tuple)):
            out.extend(_flat(x))
        else:
            out.append(x)
    return out


class Buf:
    __slots__ = ("name", "w", "r", "sem", "cnt", "tot", "excl")

    def __init__(self, name):
        self.name = name
        self.excl = False
        self.w = None
        self.r = []
        self.sem = None
        self.cnt = 0
        self.tot = 0


class Op:
    __slots__ = ("eng", "fn", "deps", "sig", "sigidx", "key", "waits", "tag")


class Prog:
    ENG = ("pe", "act", "dve", "pool", "sp")

    def __init__(self):
        self.ops = {e: [] for e in self.ENG}
        self.all = []
        self.bar_ops = []
        self.bar_pending = set()
        self.dma_last = {}
        self.tag = ""
        self.last_bar = {}

    def barrier(self):
        ops = [self.last_bar[e] for e in self.ENG if e in self.last_bar]
        ops += list(self.dma_last.values())
        self.bar_ops = ops
        self.bar_pending = set(self.ENG)
        self.dma_last = {}

    def add(self, eng, fn, reads=(), writes=(), key=None, nobar=False):
        op = Op()
        op.eng = eng
        op.fn = fn
        op.key = key
        op.sig = False
        op.sigidx = 0
        op.waits = []
        op.tag = self.tag
        reads = _flat(reads)
        writes = _flat(writes)
        ex = [b for b in reads if b.excl]
        if ex:
            reads = [b for b in reads if not b.excl]
            writes = list(writes) + [b for b in ex if b not in writes]
        deps = []
        if not nobar:
            if eng in self.bar_pending:
                deps.extend(self.bar_ops)
                self.bar_pending.discard(eng)
            if key is not None:
                self.dma_last[id(key)] = op
            self.last_bar[eng] = op
        for b in reads:
            if b.w is not None:
                deps.append(b.w)
        for b in writes:
            if b.w is not None:
                deps.append(b.w)
            deps.extend(b.r)
        dd = []
        seen = set()
        for d in deps:
            if id(d) in seen or d is op:
                continue
            seen.add(id(d))
            if d.eng == "pe" and eng == "pe" and d.key is None and key is None:
                continue
            if key is not None and d.key is key and d.eng == eng:
                continue
            dd.append(d)
        op.deps = dd
        for b in reads:
            b.r = [r for r in b.r if not (r.eng == eng and (r.key is None) == (key is None) and key is None)]
            b.r.append(op)
        for b in writes:
            b.w = op
            b.r = []
        self.ops[eng].append(op)
        self.all.append(op)
        return op

    def finalize(self, nc, stack):
        for op in self.all:
            for d in op.deps:
                if d.key is None:
                    d.sig = True
        self.sems = {}
        for e in self.ENG:
            self.sems[e] = stack.enter_context(nc.semaphore("sem_" + e))
            c = 0
            for op in self.ops[e]:
                if op.sig and op.key is None:
                    c += 1
                    op.sigidx = c
        waited = {e: {} for e in self.ENG}
        keys = []
        for op in self.all:
            w = waited[op.eng]
            for d in op.deps:
                if d.key is None:
                    sid, sem, val = ("E", d.eng), self.sems[d.eng], d.sigidx
                else:
                    k = d.key
                    if k.sem is None:
                        k.sem = stack.enter_context(nc.semaphore("sem_d%d" % len(keys)))
                        keys.append(k)
                    sid, sem, val = ("K", id(k)), k.sem, 16 * k.cnt
                if w.get(sid, 0) >= val:
                    continue
                w[sid] = val
                op.waits.append((sem, val))
            if op.key is not None:
                k = op.key
                if k.sem is None:
                    k.sem = stack.enter_context(nc.semaphore("sem_d%d" % len(keys)))
                    keys.append(k)
                k.cnt += 1
        self.keys = keys

    def emit(self, eng_name, eng, final_waits=()):
        sem_e = self.sems[eng_name]
        for op in self.ops[eng_name]:
            for sem, val in op.waits:
                eng.wait_ge(sem, val)
            ins = op.fn(eng)
            if op.key is not None:
                ins.then_inc(op.key.sem, 16)
            elif op.sig:
                ins.then_inc(sem_e, 1)
        for sem, val in final_waits:
            eng.wait_ge(sem, val)


def build(depth=DEPTH, stop=None):
    nc = bass.Bass("TRN2", target_bir_lowering=False)
    P = Prog()
    stack = ExitStack()

    def din(name, shape):
        return nc.dram_tensor(name, list(shape), F32, kind="ExternalInput").ap()

    def dout(name, shape):
        return nc.dram_tensor(name, list(shape), F32, kind="ExternalOutput").ap()

    xT_d = din("xT", [D, NT])
    cT_d = din("cT", [128, KC * 17])
    cache_d = [din("cache0", [DEPTH, 16, 128, 512]), din("cache1", [DEPTH, 16, 512, 512]),
               din("cache2", [DEPTH, 16, 2048, 512])]
    convst_d = din("convst", [DEPTH, 128, FC * 32])
    ada_w_d = din("ada_w", [DEPTH, D, 6 * D])
    w_in_d = din("w_in", [DEPTH, D, IN_W])
    w_a2d_d = din("w_a2d", [DEPTH, 512, D])
    w_b2d_d = din("w_b2d", [DEPTH, 256, D])
    w_out_d = din("w_out", [DEPTH, D, D])
    w_up_d = din("w_up", [DEPTH, D, 2 * DFF])
    w_down_d = din("w_down", [DEPTH, DFF, D])
    vecs_d = din("vecs", [128, NV])
    lnv_d = din("lnv", [DEPTH, 128, 1792])
    wspT_d = din("wspT", [DEPTH, 128, 512])
    wsps_d = din("wsps", [DEPTH, 64, 256])
    consts_d = din("consts", [128, NCON])

    yT_d = dout("yT", [D, NT])
    kvp_d = [dout("kvp0", [DEPTH, 128, 512]), dout("kvp1", [DEPTH, 512, 512]), dout("kvp2", [DEPTH, 2048, 512])]
    convp_d = dout("convp", [DEPTH, 128, FC * 2])
    kvs_d = dout("kvs", [DEPTH, 3, 64, 512])
    convs_d = dout("convs", [DEPTH, 128, FC * 32])
    vns_d = dout("vns", [DEPTH, 64, 512])

    def sb(name, shape, dt):
        return stack.enter_context(nc.sbuf_tensor(name, list(shape), dt))

    xT = sb("xT_sb", [128, KC, NT], F32)
    hT = sb("hT_sb", [128, KC * NT], BF16)
    AR1 = 17280
    arena = sb("arena", [128, AR1], BF16)
    AR2 = 29184 // 2
    arena2 = sb("arena2", [128, AR2], BF16)
    ring = sb("ring", [128, RING_UNITS * RING_UNIT], BF16)
    vecs = sb("vecs_sb", [128, NV], F32)
    constb = sb("constb", [128, NCON], BF16)
    ones = sb("ones", [128, 128], BF16)
    scT = sb("scT", [128, KC, 17], BF16)
    shT = sb("shT", [128, 2, KC, 17], F32)
    gs1 = sb("gs1", [128, KC, 17], F32)
    gg1 = sb("gg1", [128, KC, 17], F32)
    gs2 = sb("gs2", [128, KC, 17], F32)
    gg2 = sb("gg2", [128, KC, 17], F32)
    smallreg = sb("smallreg", [128, 6656], BF16)
    actx = sb("actx", [128, 1088], BF16)
    lnv = smallreg[:, 0:3584].bitcast(F32)
    wspb = smallreg[:, 3584:4096]
    wssb = smallreg[0:64, 4096:4352]
    qsT = smallreg[:, 4352:4736].rearrange("p (a q) -> p a q", a=6)
    ksT = smallreg[:, 4736:5120].rearrange("p (a q) -> p a q", a=6)
    qblk = smallreg[:, 5120:5888].rearrange("p (a b c) -> p a b c", a=6, b=16)
    vs_b = smallreg[0:64, 5888:6656].rearrange("p (a c) -> p a c", a=6)
    kvs_st = sb("kvs_st", [64, 256], F32)
    halo = sb("halo", [128, FC, 2], F32)
    convp_st = sb("convp_st", [128, FC, 2], F32)
    small = sb("small", [128, 16], F32)

    banks = [stack.enter_context(nc.psum_tensor("ps%d" % i, [128, 512], F32)) for i in range(8)]
    bankB = [Buf("bank%d" % i) for i in range(8)]
    for b_ in bankB:
        b_.excl = True
    bstate = {"next": 0, "pinned": set()}

    def nbank(pin=False):
        while True:
            i = bstate["next"]
            bstate["next"] = (i + 1) % 8
            if i not in bstate["pinned"]:
                break
        if pin:
            bstate["pinned"].add(i)
        return i

    def unpin(i):
        bstate["pinned"].discard(i)

    cT = arena2[:, 2048:2320].bitcast(F32).rearrange("p (c j) -> p c j", c=KC)
    modT = arena2[:, 0:1632].bitcast(F32).rearrange("p (m j) -> p m j", j=17)
    hTv = hT[:].rearrange("p (c t) -> p c t", c=KC)
    yaT = arena[:, 0:4 * NT].rearrange("p (c t) -> p c t", c=4)
    ybT = arena[:, 4 * NT:6 * NT].rearrange("p (c t) -> p c t", c=2)
    MG0 = 6 * NT
    GMAX = 1088
    def merged_c(m):
        if m < 4:
            return arena[:, MG0 + m * GMAX:MG0 + (m + 1) * GMAX]
        return smallreg[:, (m - 4) * GMAX:(m - 3) * GMAX]

    def act_c(c):
        if c < 15:
            return arena[:, c * GMAX:(c + 1) * GMAX]
        if c < 21:
            return smallreg[:, (c - 15) * GMAX:(c - 14) * GMAX]
        return actx[:, :]
    TA = MG0
    uacc = arena[:, TA:TA + 4096].bitcast(F32)
    qT = arena2[:, 0:2048]
    kT = arena2[:, 2048:4096]
    vtok = arena2[:, 4096:6144].rearrange("p (b c) -> p b c", b=16)
    A2T = 6144
    ug = arena2[:, A2T:A2T + 2048].rearrange("p (c t) -> p c t", c=4)
    vg_f = arena2[:, A2T + 2048:A2T + 3072].bitcast(F32)
    z_f = arena2[:, A2T + 3072:A2T + 4096].bitcast(F32)
    vn_f = arena2[:, A2T + 4096:A2T + 5120].bitcast(F32)
    vn_b = arena2[:, A2T + 5120:A2T + 5632]
    pex = [arena2[:, A2T + i * 256:A2T + (i + 1) * 256] for i in range(4)]
    pT = [arena2[:, A2T + 1024 + i * 256:A2T + 1024 + (i + 1) * 256] for i in range(4)]
    kvst = [arena2[:, A2T + 2048 + i * 512:A2T + 2048 + (i + 1) * 512].bitcast(F32) for i in range(2)]
    zacc = arena2[:, A2T + 3072:A2T + 3072 + 4096].bitcast(F32)
    NKV = 5
    kvc = [arena2[:, i * 2048:(i + 1) * 2048].rearrange("p (t c) -> p t c", c=512) for i in range(NKV)]
    KB0 = NKV * 2048
    kcT = [arena2[:, KB0 + i * 1024:KB0 + (i + 1) * 1024] for i in range(2)]
    pcs = [arena2[:, KB0 + 2048 + i * 64:KB0 + 2048 + (i + 1) * 64] for i in range(2)]
    pcm = [arena2[:, KB0 + 2176 + i * 64:KB0 + 2176 + (i + 1) * 64] for i in range(2)]
    pnew = arena2[:, KB0 + 2304:KB0 + 2304 + 3 * 256].rearrange("p (g h q) -> p g h q", g=3, h=4)
    pnx = arena2[:, KB0 + 3072:KB0 + 3072 + 256]
    rzs = arena2[:, KB0 + 3328:KB0 + 3328 + 256].bitcast(F32)
    assert KB0 + 3584 <= AR2
    NT0 = AR2 - 4096
    normD = (arena2[:, NT0:NT0 + 1024].rearrange("p (s t) -> p s t", s=2),
             arena2[:, NT0 + 1024:NT0 + 3072].bitcast(F32).rearrange("p (s t) -> p s t", s=2),
             arena2[:, NT0 + 3072:NT0 + 4096].bitcast(F32))
    normC = (arena2[:, 8192:9216].rearrange("p (s t) -> p s t", s=2),
             arena2[:, 10240:12288].bitcast(F32).rearrange("p (s t) -> p s t", s=2),
             arena2[:, 9216:10240].bitcast(F32))
    ytile_c = arena2[:, 0:8192].bitcast(F32).rearrange("p (c t) -> p c t", c=KC)
    ga_t = arena2[:, 8192:9216].bitcast(F32)
    gb_t = arena2[:, 9216:10240].bitcast(F32)
    t1_t = arena2[:, 10240:11264].bitcast(F32)
    t2_t = arena2[:, 11264:12288].bitcast(F32)
    H2N = KC * GMAX
    h2T = hT[:, 0:H2N].rearrange("p (c t) -> p c t", c=KC)
    ytile_d = hT[:, H2N:H2N + 8192].bitcast(F32).rearrange("p (c t) -> p c t", c=KC)
    GB = 2 + GMAX
    gbuf = [arena2[:, i * 2 * GB:(i + 1) * 2 * GB].bitcast(F32) for i in range(2)]
    acc_t = [arena2[:, 4 * GB + i * 1024:4 * GB + (i + 1) * 1024].bitcast(F32) for i in range(2)]
    ge_t = [arena2[:, 4 * GB + 2048 + i * 1024:4 * GB + 2048 + (i + 1) * 1024].bitcast(F32) for i in range(2)]
    gbs = [arena2[:, 4 * GB + 4096 + i * 192:4 * GB + 4096 + (i + 1) * 192].bitcast(F32).rearrange("p (j b) -> p j b", j=6) for i in range(2)]
    assert 4 * GB + 4096 + 384 <= NT0

    B = {}

    def bf(name):
        if name not in B:
            B[name] = Buf(name)
        return B[name]

    xB = [[bf("x%d_%d" % (c, t)) for t in range(5)] for c in range(KC)]
    hB = [[bf("h%d_%d" % (c, t)) for t in range(5)] for c in range(KC)]
    yaB = [bf("ya%d" % j) for j in range(17)]
    ybB = [[bf("yb%d_%d" % (p, s)) for s in range(2)] for p in range(2)]
    ringB = [bf("ring%d" % i) for i in range(RING_UNITS)]
    xload = bf("xload")
    misc_ld = bf("misc_ld")
    ystore = bf("ystore")
    out_keys = []

    def act(out, in_, func, reads, writes, bias=None, scale=None):
        kw = {}
        if bias is not None:
            kw["bias"] = bias
        if scale is not None:
            kw["scale"] = scale
        return P.add("act", lambda e: e.activation(out=out, in_=in_, func=func, **kw), reads, writes)

    def tt(eng, out, in0, in1, op, reads, writes):
        return P.add(eng, lambda e: e.tensor_tensor(out=out, in0=in0, in1=in1, op=op), reads, writes)

    def ts(eng, out, in0, s1, s2, op0, op1, reads, writes):
        if s2 is None:
            return P.add(eng, lambda e: e.tensor_scalar(out=out, in0=in0, scalar1=s1, scalar2=None, op0=op0), reads, writes)
        return P.add(eng, lambda e: e.tensor_scalar(out=out, in0=in0, scalar1=s1, scalar2=s2, op0=op0, op1=op1), reads, writes)

    def stt(out, in0, scalar, in1, op0, op1, reads, writes):
        return P.add("dve", lambda e: e.scalar_tensor_tensor(out=out, in0=in0, scalar=scalar, in1=in1, op0=op0, op1=op1), reads, writes)

    def cp(eng, out, in_, reads, writes):
        if eng == "act":
            return P.add("act", lambda e: e.copy(out=out, in_=in_), reads, writes)
        return P.add(eng, lambda e: e.tensor_copy(out=out, in_=in_), reads, writes)

    def mm(out, lhsT, rhs, start, stop, reads, bank, skip=False):
        return P.add("pe", lambda e: e.matmul(out, lhsT, rhs, start=start, stop=stop, skip_group_check=skip),
                     reads, [bankB[bank]])

    def dma(eng, out, in_, reads, writes, key, nobar=False):
        return P.add(eng, lambda e: e.dma_start(out=out, in_=in_), reads, writes, key=key, nobar=nobar)

    jobs = []
    mstate = {}

    tagbox = ["start"]

    def job(parts, fn):
        jobs.append((parts, fn, tagbox[0]))

    def settag(t):
        tagbox[0] = t

    def run_jobs():
        wjobs = [i for i, (p, f, _) in enumerate(jobs) if p is not None]
        pos = {j: n for n, j in enumerate(wjobs)}
        views = {}
        allocs = {}
        state = {"head": 0, "issued": 0}

        def size_of(parts):
            return sum(kc * ncol for (_, kc, ncol) in parts)

        def try_issue(cur):
            n = state["issued"]
            if n >= len(wjobs):
                return False
            j = wjobs[n]
            parts = jobs[j][0]
            nu = -(-size_of(parts) // RING_UNIT)
            assert nu <= RING_UNITS
            head = state["head"]
            if head + nu > RING_UNITS:
                head = 0
            units = set(range(head, head + nu))
            for m in range(cur, n):
                if allocs[m] & units:
                    return False
            allocs[n] = units
            state["head"] = head + nu
            state["issued"] = n + 1
            bufs = [ringB[u] for u in sorted(units)]
            key = bufs[0]
            base = head * RING_UNIT
            vs = WS()
            kcs = set(p[1] for p in parts)
            off = 0
            if len(kcs) == 1:
                kc = parts[0][1]
                tot = sum(p[2] for p in parts)
                whole = ring[:, base:base + kc * tot].rearrange("p (k n) -> p k n", k=kc)
                for (src, _, ncol) in parts:
                    dst = whole[:, :, off:off + ncol]
                    dma("pool", dst, src, [], bufs, key, nobar=True)
                    vs.append(dst)
                    off += ncol
                vs.whole = whole
            else:
                for (src, kc, ncol) in parts:
                    dst = ring[:, base + off:base + off + kc * ncol].rearrange("p (k n) -> p k n", k=kc)
                    dma("pool", dst, src, [], bufs, key, nobar=True)
                    vs.append(dst)
                    off += kc * ncol
            views[j] = (vs, bufs)
            return True

        for i, (parts, fn, _) in enumerate(jobs):
            P.tag = jobs[i][2]
            if parts is not None:
                n = pos[i]
                while try_issue(n):
                    pass
                assert i in views, "ring too small for chunk"
                fn(*views.pop(i))
                allocs.pop(n, None)
            else:
                fn()

    def wview(dram, l, r0, nrows, c0, ncol):
        kc = nrows // 128
        return (dram[l, r0:r0 + nrows, c0:c0 + ncol].rearrange("(k p) n -> p k n", p=128), kc, ncol)

    for c in range(KC):
        dma("sp", xT[:, c, :], xT_d[c * 128:(c + 1) * 128, :], [], [xB[c][t] for t in range(5)], xload)
    dma("sp", vecs[:], vecs_d[:, :], [], [bf("vecs")], misc_ld)
    dma("sp", cT.rearrange("p c j -> p (c j)"), cT_d[:, :], [], [bf("cT")], misc_ld)
    dma("pool", constb[:], consts_d[:, :], [], [bf("constb")], bf("constb"))
    P.add("dve", lambda e: e.memset(ones[:], 1.0), [], [bf("ones")])
    act(scT[:].rearrange("p c j -> p (c j)"), cT.rearrange("p c j -> p (c j)"), AF.Silu, [bf("cT")], [bf("scT")])
    P.barrier()

    identb = constb[:, C_ID:C_ID + 128]
    mask2 = constb[:, C_MPREV:C_MPREV + 256]
    mask_own = constb[:, C_MOWN:C_MOWN + 128]

    def rms_to_h_group(l, which, tiles, dst, dstB, rstd_all, nt=None):
        sq_t, tmp_t, _ = nt or normD
        gsv = gs1 if which == 0 else gs2
        off = {}
        o = 0
        for ti in tiles:
            off[ti] = o
            o += TILES[ti][1]
        for ti in tiles:
            t0, n = TILES[ti]
            rs = rstd_all[:, off[ti]:off[ti] + n]
            rB = bf("rstdg%d" % ti)
            bk = nbank()
            for c in range(KC):
                s = c % 2
                act(sq_t[:, s, :n], xT[:, c, t0:t0 + n], AF.Square, [xB[c][ti]], [bf("sq%d" % s)])
                mm(banks[bk][:, :n], ones[:, :], sq_t[:, s, :n], c == 0, c == KC - 1, [bf("sq%d" % s), bf("ones")], bk)
            act(rs, banks[bk][:, :n], AF.Sqrt, [bankB[bk]], [rB], bias=EPS, scale=1.0 / D)
            P.add("dve", lambda e, rs=rs: e.reciprocal(out=rs, in_=rs), [rB], [rB])
        for ti in tiles:
            t0, n = TILES[ti]
            rs = rstd_all[:, off[ti]:off[ti] + n]
            rB = bf("rstdg%d" % ti)
            for c in range(KC):
                s = c % 2
                tt("dve" if ti < 4 else "pool", tmp_t[:, s, :n], xT[:, c, t0:t0 + n], rs, ALU.mult,
                   [xB[c][ti], rB], [bf("tmp%d" % s)])
                if ti < 4:
                    act(dst(c, t0, n), tmp_t[:, s, :n], AF.Identity, [bf("tmp%d" % s), bf("mod")], [dstB(c, ti)],
                        bias=shT[:, which, c, 0:1], scale=gsv[:, c, 0:1])
                else:
                    t3 = tmp_t[:, s, :n].rearrange("p (b t) -> p b t", t=4)
                    tt("dve", t3, t3, gsv[:, c, 1:17].unsqueeze(2).broadcast_to([128, 16, 4]), ALU.mult,
                       [bf("tmp%d" % s), bf("mod")], [bf("tmp%d" % s)])
                    tt("pool", dst(c, t0, n).rearrange("p (b t) -> p b t", t=4), t3,
                       shT[:, which, c, 1:17].unsqueeze(2).broadcast_to([128, 16, 4]), ALU.add,
                       [bf("tmp%d" % s), bf("mod")], [dstB(c, ti)])

    on_state = {}

    def out_norm_chunk(l, which, ti, mp, bk, ytile, ytB, nt=None):
        sq_t, tmp_t, rstd_t = nt or normD
        t0, n = TILES[ti]
        if mp == 0:
            on_state["bank"] = nbank(pin=True)
        sb_ = on_state["bank"]
        s = mp % 2
        act(sq_t[:, s, :n], banks[bk][:, :n], AF.Square, [bankB[bk]], [bf("sq%d" % s)])
        cp("dve", ytile[:, mp, :n], banks[bk][:, :n], [bankB[bk]], [ytB[mp]])
        if mp >= 1:
            s1 = (mp - 1) % 2
            mm(banks[sb_][:, :n], ones[:, :], sq_t[:, s1, :n], mp == 1, False, [bf("sq%d" % s1), bf("ones")], sb_)
        if mp < KC - 1:
            return
        mm(banks[sb_][:, :n], ones[:, :], sq_t[:, s, :n], False, True, [bf("sq%d" % s), bf("ones")], sb_)
        ggv = gg1 if which == 0 else gg2
        act(rstd_t[:, :n], banks[sb_][:, :n], AF.Sqrt, [bankB[sb_]], [bf("rstd")], bias=EPS, scale=1.0 / D)
        unpin(sb_)
        P.add("dve", lambda e: e.reciprocal(out=rstd_t[:, :n], in_=rstd_t[:, :n]), [bf("rstd")], [bf("rstd")])
        for c in range(KC):
            s = c % 2
            tt("pool" if c % 2 == 0 else "dve", tmp_t[:, s, :n], ytile[:, c, :n], rstd_t[:, :n], ALU.mult,
               [ytB[c], bf("rstd")], [bf("tmp%d" % s)])
            if ti < 4:
                stt(xT[:, c, t0:t0 + n], tmp_t[:, s, :n], ggv[:, c, 0:1], xT[:, c, t0:t0 + n], ALU.mult, ALU.add,
                    [bf("tmp%d" % s), bf("mod"), xB[c][ti]], [xB[c][ti]])
            else:
                t3 = tmp_t[:, s, :n].rearrange("p (b t) -> p b t", t=4)
                tt("dve", t3, t3, ggv[:, c, 1:17].unsqueeze(2).broadcast_to([128, 16, 4]), ALU.mult,
                   [bf("tmp%d" % s), bf("mod")], [bf("tmp%d" % s)])
                tt("dve", xT[:, c, t0:t0 + n], xT[:, c, t0:t0 + n], tmp_t[:, s, :n], ALU.add,
                   [bf("tmp%d" % s), xB[c][ti]], [xB[c][ti]])

    def layer(l):
        vb = lambda off, n: vecs[:, off:off + n]

        settag('M')
        def m_chunk(lm, j):
            def fn(ws, wb):
                w = ws[0]
                mb = mstate.setdefault(lm, {})
                for m in range(2):
                    mi = j * 2 + m
                    if mi % 24 == 0:
                        mb[mi // 24] = nbank(pin=True)
                    bk = mb[mi // 24]
                    col = (mi % 24) * 17
                    for kc in range(KC):
                        mm(banks[bk][:, col:col + 17], w[:, kc, m * 128:(m + 1) * 128], scT[:, kc, :],
                           kc == 0, kc == KC - 1, [wb, bf("scT")], bk)
            return fn

        def m_job(lm, j):
            job([wview(ada_w_d, lm, 0, D, j * 256, 256)], m_chunk(lm, j))

        if l == 0:
            for j in range(24):
                m_job(0, j)

        def m_evac():
            for h in range(2):
                bk = mstate[l][h]
                tt("dve", modT[:, h * 24:(h + 1) * 24, :],
                   banks[bk][:, 0:408].rearrange("p (m j) -> p m j", j=17),
                   vecs[:, V_ADAB + l * 48 + h * 24:V_ADAB + l * 48 + (h + 1) * 24].unsqueeze(2).broadcast_to([128, 24, 17]),
                   ALU.add, [bankB[bk], bf("vecs")], [bf("mod")])
                unpin(bk)

        job(None, m_evac)

        def m_derive():
            ng = lambda k: vecs[:, V_NORMG + l * 32 + k * 8:V_NORMG + l * 32 + (k + 1) * 8].unsqueeze(2).broadcast_to([128, 8, 17])
            ts("dve", gs1[:], modT[:, 8:16, :], 1.0, None, ALU.add, None, [bf("mod")], [bf("mod")])
            tt("dve", gs1[:], gs1[:], ng(0), ALU.mult, [bf("mod"), bf("vecs")], [bf("mod")])
            tt("dve", gg1[:], modT[:, 16:24, :], ng(1), ALU.mult, [bf("mod"), bf("vecs")], [bf("mod")])
            ts("dve", gs2[:], modT[:, 32:40, :], 1.0, None, ALU.add, None, [bf("mod")], [bf("mod")])
            tt("dve", gs2[:], gs2[:], ng(2), ALU.mult, [bf("mod"), bf("vecs")], [bf("mod")])
            tt("dve", gg2[:], modT[:, 40:48, :], ng(3), ALU.mult, [bf("mod"), bf("vecs")], [bf("mod")])
            cp("dve", shT[:, 0, :, :], modT[:, 0:8, :], [bf("mod")], [bf("mod")])
            cp("dve", shT[:, 1, :, :], modT[:, 24:32, :], [bf("mod")], [bf("mod")])
            dma("sp", lnv, lnv_d[l], [], [bf("lnv")], bf("lnv"))
            dma("pool", wspb, wspT_d[l], [], [bf("wspb")], bf("wspb"))
            dma("pool", wssb, wsps_d[l], [], [bf("wssb")], bf("wssb"))
            P.add("dve", lambda e: e.memset(smallreg[:, 5120:5888], 0.0), [], [bf("qblk")])
            tt("dve", wspb.rearrange("p (g t) -> p g t", g=4), wspb.rearrange("p (g t) -> p g t", g=4),
               mask_own.unsqueeze(1).broadcast_to([128, 4, 128]), ALU.mult, [bf("wspb"), bf("constb")], [bf("wspb")])
            tt("dve", wssb.rearrange("p (g t) -> p g t", g=4), wssb.rearrange("p (g t) -> p g t", g=4),
               constb[0:64, C_SSP:C_SSP + 64].unsqueeze(1).broadcast_to([64, 4, 64]), ALU.mult,
               [bf("wssb"), bf("constb")], [bf("wssb")])

        job(None, m_derive)

        settag('N1')
        def n1():
            rms_to_h_group(l, 0, list(range(5)), lambda c, t0, n: hTv[:, c, t0:t0 + n], lambda c, ti_: hB[c][ti_],
                           arena2[:, 2048:2048 + 2 * NT].bitcast(F32))

        job(None, n1)

        settag('A')
        def phase_a1(ws, wb):
            wu = ws[0]
            for ti in range(5):
                t0, n = TILES[ti]
                yab = [yaB[ti * 4 + s_] for s_ in range(4)] if ti < 4 else [yaB[16]]
                for cg in range(4):
                    bk = nbank()
                    for kc in range(KC):
                        mm(banks[bk][:, :n], wu[:, kc, cg * 128:(cg + 1) * 128], hTv[:, kc, t0:t0 + n],
                           kc == 0, kc == KC - 1, [wb, hB[kc][ti]], bk)
                    act(yaT[:, cg, t0:t0 + n], banks[bk][:, :n], AF.Gelu_apprx_tanh, [bankB[bk]], yab)

        def phase_a2(ws, wb):
            wv = ws[0]
            chunks = []
            for ti in range(5):
                for sj in range(4 if ti < 4 else 1):
                    chunks.append((ti, sj))
            vgs = [vg_f, arena2[:, A2T:A2T + 1024].bitcast(F32)]
            sts = [small, arena2[:, A2T + 1024:A2T + 1056].bitcast(F32)]

            def stage1(k):
                ti, sj = chunks[k]
                t0, n = TILES[ti]
                par = k % 2
                vg, st = vgs[par], sts[par]
                pn = 128 if ti < 4 else 64
                c0 = t0 + sj * 128
                bk = nbank()
                for kc in range(KC):
                    mm(banks[bk][:pn, :], hTv[:, kc, c0:c0 + pn], wv[:, kc, :], kc == 0, kc == KC - 1,
                       [wb, hB[kc][ti]], bk)
                act(vg[:pn, :], banks[bk][:pn, :], AF.Gelu_apprx_tanh, [bankB[bk]], [bf("vg%d" % par)])
                P.add("dve", lambda e: e.bn_stats(out=st[:pn, 0:6], in_=vg[:pn, :]), [bf("vg%d" % par)], [bf("bnst%d" % par)])
                P.add("dve", lambda e: e.bn_aggr(out=st[:pn, 6:8], in_=st[:pn, 0:6]), [bf("bnst%d" % par)], [bf("bnag%d" % par)])
                act(st[:pn, 8:9], st[:pn, 7:8], AF.Sqrt, [bf("bnag%d" % par)], [bf("lnr%d" % par)], bias=EPS, scale=1.0)
                P.add("dve", lambda e: e.reciprocal(out=st[:pn, 9:10], in_=st[:pn, 8:9]), [bf("lnr%d" % par)], [bf("lnr2%d" % par)])

            def stage2(k):
                ti, sj = chunks[k]
                t0, n = TILES[ti]
                par = k % 2
                vg, st = vgs[par], sts[par]
                j = ti * 4 + sj
                pn = 128 if ti < 4 else 64
                c0 = t0 + sj * 128
                ts("dve", z_f[:pn, :], vg[:pn, :], st[:pn, 6:7], st[:pn, 9:10], ALU.subtract, ALU.mult,
                   [bf("vg%d" % par), bf("bnag%d" % par), bf("lnr2%d" % par)], [bf("zf")])
                tt("dve", z_f[:pn, :], z_f[:pn, :], lnv[:pn, 0:512], ALU.mult, [bf("zf"), bf("lnv")], [bf("zf")])
                if ti < 4:
                    tt("pool", vn_b[:pn, :], z_f[:pn, :], lnv[:pn, 512:1024], ALU.add, [bf("zf"), bf("lnv")], [bf("vnb")])
                else:
                    tt("pool", vn_f[:pn, :], z_f[:pn, :], lnv[:pn, 512:1024], ALU.add, [bf("zf"), bf("lnv")], [bf("vnf")])
                    cp("dve", vn_b[:pn, :], vn_f[:pn, :], [bf("vnf")], [bf("vnb")])
                    dma("sp", vns_d[l], vn_f[:pn, :], [bf("vnf")], [], bf("vnf"))
                    if bf("vnf") not in out_keys:
                        out_keys.append(bf("vnf"))
                bk2 = nbank()
                for g in range(4):
                    if ti < 4:
                        o = banks[bk2][:, g * 128:(g + 1) * 128]
                        mm(o, vn_b[:, g * 128:(g + 1) * 128], wspb[:, g * 128:(g + 1) * 128], True, True,
                           [bf("vnb"), bf("wspb")], bk2)
                    else:
                        o = banks[bk2][:, g * 64:(g + 1) * 64]
                        mm(o, vn_b[0:64, g * 128:(g + 1) * 128], wssb[0:64, g * 64:(g + 1) * 64], True, True,
                           [bf("vnb"), bf("wssb")], bk2)
                boff = 1024 if ti < 4 else 1536
                tt("dve", z_f[:, 0:4 * pn], banks[bk2][:, 0:4 * pn], lnv[:, boff:boff + 4 * pn], ALU.add,
                   [bankB[bk2], bf("lnv")], [bf("zf")])
                tt("dve", yaT[:, :, c0:c0 + pn], yaT[:, :, c0:c0 + pn],
                   z_f[:, 0:4 * pn].rearrange("p (g t) -> p g t", g=4), ALU.mult,
                   [yaB[j], bf("zf")], [yaB[j]])

            stage1(0)
            for k in range(len(chunks)):
                if k + 1 < len(chunks):
                    stage1(k + 1)
                stage2(k)

        job(None, P.barrier)
        job([wview(w_in_d, l, 0, D, 0, 512)], phase_a1)
        job([wview(w_in_d, l, 0, D, 512, 512)], phase_a2)
        job(None, P.barrier)

        settag('B')
        def phase_b(pair, g):
            dil = DILS[g]
            nsub = 16 // dil
            sidx = g * 2 + pair

            def fn(ws, wb):
                w = ws.whole
                for which in range(2):
                    dstT = qT if which == 0 else kT
                    for ti in range(5):
                        t0, n = TILES[ti]
                        bk = nbank()
                        for kc in range(KC):
                            mm(banks[bk][:, :n], w[:, kc, which * 128:(which + 1) * 128], hTv[:, kc, t0:t0 + n],
                               kc == 0, kc == KC - 1, [wb, hB[kc][ti]], bk)
                        if ti < 4:
                            src = banks[bk][:, 0:512].rearrange("p (n r) -> p r n", r=dil)
                            dst = dstT.rearrange("p (r n) -> p r n", r=dil)[:, :, ti * (512 // dil):(ti + 1) * (512 // dil)]
                            cp("act" if which == 0 else "dve", dst, src, [bankB[bk]], [bf("qT" if which == 0 else "kT")])
                        elif which == 0:
                            cp("act", qsT[:, sidx, :], banks[bk][:, 0:64], [bankB[bk]], [bf("qsT")])
                            cp("dve", qblk[0:64, sidx, :, 0:4], banks[bk][0:64, 0:64].rearrange("p (b t) -> p b t", t=4),
                               [bankB[bk]], [bf("qblk")])
                            cp("dve", qblk[64:128, sidx, :, 4:8], banks[bk][64:128, 0:64].rearrange("p (b t) -> p b t", t=4),
                               [bankB[bk]], [bf("qblk")])
                        else:
                            cp("act", ksT[:, sidx, :], banks[bk][:, 0:64], [bankB[bk]], [bf("ksT")])
                keep_from = {0: NP_ - 128, 1: NP_ - 512, 2: 0}[g]
                for blk in range(17):
                    bk = nbank()
                    if blk < 16:
                        r, nb = blk // nsub, blk % nsub
                        start = r + dil * 128 * nb
                        pn = 128
                        if start >= keep_from:
                            for kc in range(KC):
                                mm(banks[bk][:, 0:256], hTv[:, kc, start:start + dil * 127 + 1:dil], w[:, kc, 128:384],
                                   kc == 0, kc == KC - 1, [wb] + [hB[kc][t] for t in range(4)], bk)
                        else:
                            for kc in range(KC):
                                mm(banks[bk][:, 128:256], hTv[:, kc, start:start + dil * 127 + 1:dil], w[:, kc, 256:384],
                                   kc == 0, kc == KC - 1, [wb] + [hB[kc][t] for t in range(4)], bk)
                    else:
                        pn = 64
                        for kc in range(KC):
                            mm(banks[bk][:64, 0:256], hTv[:, kc, NP_:NT], w[:, kc, 128:384],
                               kc == 0, kc == KC - 1, [wb, hB[kc][4]], bk)
                    if blk < 16:
                        s = blk % 2
                        kept = start >= keep_from
                        if kept:
                            cp("act", kvst[s][:, :], banks[bk][:, 0:256], [bankB[bk]], [bf("kvst%d" % s)])
                            dst = kvp_d[g][l, start - keep_from:start - keep_from + dil * 127 + 1:dil, :]
                            dst = dst.rearrange("t (kv c) -> t kv c", kv=2)[:, :, pair * 128:(pair + 1) * 128]
                            dma("sp", dst, kvst[s][:, :].rearrange("p (kv c) -> p kv c", kv=2), [bf("kvst%d" % s)], [],
                                bf("kvst%d" % s))
                            if bf("kvst%d" % s) not in out_keys:
                                out_keys.append(bf("kvst%d" % s))
                        cp("dve", vtok[:, blk, :], banks[bk][:, 128:256], [bankB[bk]], [bf("vtok")])
                    else:
                        cp("act", kvs_st[:, :], banks[bk][:64, 0:256], [bankB[bk]], [bf("kvs_st")])
                        dma("sp", kvs_d[l, g].rearrange("t (kv c) -> t kv c", kv=2)[:, :, pair * 128:(pair + 1) * 128],
                            kvs_st[:, :].rearrange("p (kv c) -> p kv c", kv=2), [bf("kvs_st")], [], bf("kvs_st"))
                        if bf("kvs_st") not in out_keys:
                            out_keys.append(bf("kvs_st"))
                        cp("dve", vs_b[:, sidx, :], banks[bk][:64, 128:256], [bankB[bk]], [bf("vs_b")])
                st = {}

                def stage_s(qb):
                    r, nb = qb // nsub, qb % nsub
                    kbs = ([qb - 1] if nb > 0 else []) + [qb]
                    nk = len(kbs)
                    ba, bb = nbank(), nbank()
                    for h2, bk in ((0, ba), (1, bb)):
                        for i, kb in enumerate(kbs):
                            mm(banks[bk][:, i * 128:(i + 1) * 128], kT[h2 * 64:(h2 + 1) * 64, kb * 128:(kb + 1) * 128],
                               qT[h2 * 64:(h2 + 1) * 64, qb * 128:(qb + 1) * 128], True, True, [bf("qT"), bf("kT")], bk)
                    pts = []
                    for h2, bk in ((0, ba), (1, bb)):
                        s = (qb % 2) * 2 + h2
                        act(pex[s][:, :nk * 128], banks[bk][:, :nk * 128], AF.Exp, [bankB[bk]], [bf("pex%d" % s)], scale=0.125)
                        m = mask2 if nk == 2 else mask_own
                        tt("dve", pT[s][:, :nk * 128], pex[s][:, :nk * 128], m, ALU.mult, [bf("pex%d" % s), bf("constb")],
                           [bf("pT%d" % s)])
                        pts.append(s)
                    st[qb] = (kbs, pts)

                def stage_pv(qb):
                    kbs, pts = st.pop(qb)
                    r, nb = qb // nsub, qb % nsub
                    bu, bz = nbank(), nbank()
                    nk = len(kbs)
                    for i, kb in enumerate(kbs):
                        for h2 in range(2):
                            mm(banks[bu][h2 * 64:(h2 + 1) * 64, 0:128], vtok[:, kb, h2 * 64:(h2 + 1) * 64],
                               pT[pts[h2]][:, i * 128:(i + 1) * 128], i == 0, i == nk - 1, [bf("vtok"), bf("pT%d" % pts[h2])], bu)
                    for i, kb in enumerate(kbs):
                        for h2 in range(2):
                            mm(banks[bz][h2 * 64:(h2 + 1) * 64, 0:128], ones[:, 0:64],
                               pT[pts[h2]][:, i * 128:(i + 1) * 128], i == 0, i == nk - 1, [bf("ones"), bf("pT%d" % pts[h2])], bz)
                    start = r + dil * 128 * nb
                    sl = slice(start, start + dil * 127 + 1, dil)
                    if g == 0:
                        cp("act", uacc[:, sl], banks[bu][:, 0:128], [bankB[bu]], [bf("uacc")])
                        cp("dve", zacc[:, sl], banks[bz][:, 0:128], [bankB[bz]], [bf("zacc")])
                    else:
                        tt("dve", uacc[:, sl], uacc[:, sl], banks[bu][:, 0:128], ALU.add, [bf("uacc"), bankB[bu]], [bf("uacc")])
                        tt("dve", zacc[:, sl], zacc[:, sl], banks[bz][:, 0:128], ALU.add, [bf("zacc"), bankB[bz]], [bf("zacc")])

                for qb in range(17):
                    if qb < 16:
                        stage_s(qb)
                    if qb >= 1:
                        stage_pv(qb - 1)
                if g == 2:
                    P.add("dve", lambda e: e.reciprocal(out=zacc[:, :], in_=zacc[:, :]), [bf("zacc")], [bf("zacc")])
                    tt("dve", ybT[:, pair, 0:NP_], uacc[:, :], zacc[:, :], ALU.mult, [bf("uacc"), bf("zacc")], [ybB[pair][0]])
            return fn

        for pair in range(2):
            for g in range(3):
                cq = O0 + g * 256 + pair * 128
                job([wview(w_in_d, l, 0, D, cq, 128), wview(w_in_d, l, 0, D, cq + 768, 128),
                     wview(w_in_d, l, 0, D, cq + 1536, 128)], phase_b(pair, g))

        settag('BS')
        def phase_bs():
            bU, bZ = nbank(pin=True), nbank(pin=True)
            first = {0: True, 1: True}
            for g in range(3):
                be, bo = nbank(), nbank()
                for pair in range(2):
                    sidx = g * 2 + pair
                    for h2, bk in ((0, be), (1, bo)):
                        mm(banks[bk][0:64, pair * 64:(pair + 1) * 64], ksT[h2 * 64:(h2 + 1) * 64, sidx, :],
                           qsT[h2 * 64:(h2 + 1) * 64, sidx, :], True, True, [bf("ksT"), bf("qsT")], bk)
                msk = constb[0:64, C_SNEW0:C_SNEW0 + 64] if g == 0 else constb[0:64, C_SNEW1:C_SNEW1 + 64]
                for h2, bk in ((0, be), (1, bo)):
                    act(pnx[0:64, h2 * 128:(h2 + 1) * 128], banks[bk][0:64, 0:128], AF.Exp, [bankB[bk]], [bf("pnx")], scale=0.125)
                for h2 in range(2):
                    tt("dve", pnew[0:64, g, h2 * 2:(h2 + 1) * 2, :],
                       pnx[0:64, h2 * 128:(h2 + 1) * 128].rearrange("p (a q) -> p a q", a=2),
                       msk.unsqueeze(1).broadcast_to([64, 2, 64]), ALU.mult, [bf("pnx"), bf("constb")], [bf("pnew")])
                for pair in range(2):
                    sidx = g * 2 + pair
                    for h2 in range(2):
                        rhs = pnew[0:64, g, h2 * 2 + pair, :]
                        mm(banks[bU][h2 * 64:(h2 + 1) * 64, pair * 64:(pair + 1) * 64], vs_b[0:64, sidx, h2 * 64:(h2 + 1) * 64],
                           rhs, first[h2], False, [bf("vs_b"), bf("pnew")], bU, skip=True)
                        mm(banks[bZ][h2 * 64:(h2 + 1) * 64, pair * 64:(pair + 1) * 64], ones[0:64, 0:64],
                           rhs, first[h2], False, [bf("ones"), bf("pnew")], bZ, skip=True)
                        first[h2] = False
            units = [(g, b) for g in range(3) for b in range(16)]
            stt_ = {}

            def s_load(u):
                g, b = units[u]
                s = u % NKV
                nt = 1 if g == 0 else 4
                if g == 0:
                    src = cache_d[0][l, b].rearrange("(t i) c -> i t c", t=1)
                elif g == 1:
                    src = cache_d[1][l, b].rearrange("(i r) c -> i r c", r=4)
                else:
                    src = cache_d[2][l, b].rearrange("(i r) c -> i r c", r=16)[:, 0:4, :]
                dma("pool", kvc[s][:, 0:nt, :], src, [], [bf("kvc%d" % s)], bf("kvc%d" % s))

            def s_tr(u):
                g, b = units[u]
                s = u % NKV
                s2 = u % 2
                nt = 1 if g == 0 else 4
                bk = nbank()
                pv = banks[bk][:].bitcast(BF16)
                for t in range(nt):
                    for pair in range(2):
                        i = t * 2 + pair
                        P.add("pe", lambda e, o=pv[:, i * 128:(i + 1) * 128], a=kvc[s][:, t, pair * 128:(pair + 1) * 128]:
                              e.transpose(out=o, in_=a, identity=identb), [bf("kvc%d" % s), bf("constb")], [bankB[bk]])
                cp("act", kcT[s2][:, 0:nt * 256], pv[:, 0:nt * 256], [bankB[bk]], [bf("kcT%d" % s2)])

            def s_sc(u):
                g, b = units[u]
                s2 = u % 2
                nt = 1 if g == 0 else 4
                bk = nbank()
                for t in range(nt):
                    for pair in range(2):
                        i = t * 2 + pair
                        mm(banks[bk][:, i * 8:(i + 1) * 8], kcT[s2][:, i * 128:(i + 1) * 128], qblk[:, g * 2 + pair, b, :],
                           True, True, [bf("kcT%d" % s2), bf("qblk")], bk)
                nc_ = nt * 16
                act(pcs[s2][:, 0:nc_], banks[bk][:, 0:nc_], AF.Exp, [bankB[bk]], [bf("pcs%d" % s2)], scale=0.125)
                msk = constb[:, C_MC0:C_MC0 + 16] if g == 0 else constb[:, C_MC1:C_MC1 + 64]
                tt("dve", pcm[s2][:, 0:nc_], pcs[s2][:, 0:nc_], msk, ALU.mult, [bf("pcs%d" % s2), bf("constb")], [bf("pcm%d" % s2)])

            def s_pv(u):
                g, b = units[u]
                s = u % NKV
                s2 = u % 2
                nt = 1 if g == 0 else 4
                for t in range(nt):
                    for pair in range(2):
                        for h2 in range(2):
                            col = (t * 2 + pair) * 8 + h2 * 4
                            rhs = pcm[s2][:, col:col + 4]
                            o = slice(pair * 64 + 4 * b, pair * 64 + 4 * b + 4)
                            mm(banks[bU][h2 * 64:(h2 + 1) * 64, o], kvc[s][:, t, 256 + (pair * 2 + h2) * 64:256 + (pair * 2 + h2 + 1) * 64],
                               rhs, False, False, [bf("kvc%d" % s), bf("pcm%d" % s2)], bU, skip=True)
                            mm(banks[bZ][h2 * 64:(h2 + 1) * 64, o], ones[:, 0:64],
                               rhs, False, False, [bf("ones"), bf("pcm%d" % s2)], bZ, skip=True)

            nu = len(units)
            s_load(0)
            s_load(1)
            for step in range(nu + 3):
                if 3 <= step:
                    s_pv(step - 3)
                if step + 2 < nu:
                    s_load(step + 2)
                if 1 <= step < nu + 1:
                    s_tr(step - 1)
                if 2 <= step < nu + 2:
                    s_sc(step - 2)
            P.add("dve", lambda e: e.reciprocal(out=rzs[:, :], in_=banks[bZ][:, 0:128]), [bankB[bZ]], [bf("rzs")])
            tt("dve", ybT[:, :, NP_:NT], banks[bU][:, 0:128].rearrange("p (a q) -> p a q", a=2),
               rzs[:, :].rearrange("p (a q) -> p a q", a=2), ALU.mult, [bankB[bU], bf("rzs")], [ybB[0][1], ybB[1][1]])
            unpin(bU)
            unpin(bZ)

        job(None, P.barrier)
        job(None, phase_bs)
        job(None, P.barrier)

        settag('C')
        for gi, grp in enumerate(GROUPS):
            g0t = TILES[grp[0]][0]

            def c_chunk(m, grp=grp, g0t=g0t):
                def fn(ws, wb):
                    wga, wgb, wa, wbb = ws
                    for ti in grp:
                        t0, n = TILES[ti]
                        b1, b2, b3, b4 = nbank(), nbank(), nbank(), nbank()
                        for kc in range(KC):
                            mm(banks[b1][:, :n], wga[:, kc, :], hTv[:, kc, t0:t0 + n], kc == 0, kc == KC - 1, [wb, hB[kc][ti]], b1)
                        for kc in range(KC):
                            mm(banks[b2][:, :n], wgb[:, kc, :], hTv[:, kc, t0:t0 + n], kc == 0, kc == KC - 1, [wb, hB[kc][ti]], b2)
                        yab = [yaB[ti * 4 + s] for s in range(4)] if ti < 4 else [yaB[16]]
                        for kc in range(4):
                            mm(banks[b3][:, :n], wa[:, kc, :], yaT[:, kc, t0:t0 + n], kc == 0, kc == 3, [wb] + yab, b3)
                        ybb = [ybB[0][0], ybB[1][0]] if ti < 4 else [ybB[0][1], ybB[1][1]]
                        for kc in range(2):
                            mm(banks[b4][:, :n], wbb[:, kc, :], ybT[:, kc, t0:t0 + n], kc == 0, kc == 1, [wb] + ybb, b4)
                        act(ga_t[:, :n], banks[b1][:, :n], AF.Sigmoid, [bankB[b1]], [bf("ga")])
                        act(gb_t[:, :n], banks[b2][:, :n], AF.Sigmoid, [bankB[b2]], [bf("gb")])
                        tt("dve", t1_t[:, :n], ga_t[:, :n], banks[b3][:, :n], ALU.mult, [bf("ga"), bankB[b3]], [bf("t1")])
                        tt("dve", t2_t[:, :n], gb_t[:, :n], banks[b4][:, :n], ALU.mult, [bf("gb"), bankB[b4]], [bf("t2")])
                        tt("pool", merged_c(m)[:, t0 - g0t:t0 - g0t + n], t1_t[:, :n], t2_t[:, :n], ALU.add,
                           [bf("t1"), bf("t2")], [bf("mg%d_%d" % (m, ti))])
                return fn

            for m in range(KC):
                job([wview(w_in_d, l, 0, D, O1 + m * 128, 128), wview(w_in_d, l, 0, D, O1 + D + m * 128, 128),
                     wview(w_a2d_d, l, 0, 512, m * 128, 128), wview(w_b2d_d, l, 0, 256, m * 128, 128)], c_chunk(m))

            job(None, P.barrier)
            for ti in grp:
                def o_chunk(mp, ti=ti, g0t=g0t):
                    def fn(ws, wb):
                        w = ws[0]
                        t0, n = TILES[ti]
                        bk = nbank()
                        for m in range(KC):
                            mm(banks[bk][:, :n], w[:, m, :], merged_c(m)[:, t0 - g0t:t0 - g0t + n], m == 0, m == KC - 1,
                               [wb, bf("mg%d_%d" % (m, ti))], bk)
                        out_norm_chunk(l, 0, ti, mp, bk, ytile_c, [bf("yt%d" % c) for c in range(KC)], normC)
                    return fn
                for mp in range(KC):
                    job([wview(w_out_d, l, 0, D, mp * 128, 128)], o_chunk(mp))
            job(None, P.barrier)

        settag('D')
        job(None, P.barrier)
        cw = lambda c, j: vecs[:, V_CONVW + l * 66 + c * 3 + j:V_CONVW + l * 66 + c * 3 + j + 1]
        cb = lambda c: vecs[:, V_CONVB + l * 22 + c:V_CONVB + l * 22 + c + 1]
        for gi, grp in enumerate(GROUPS):
            g0t = TILES[grp[0]][0]

            def n2(grp=grp, g0t=g0t):
                rms_to_h_group(l, 1, grp, lambda c, t0, n: h2T[:, c, t0 - g0t:t0 - g0t + n],
                               lambda c, ti_: bf("h2_%d_%d" % (c, ti_)), arena2[:, 4 * GB:4 * GB + 2 * GMAX].bitcast(F32))

            job(None, n2)
            job(None, P.barrier)

            def d_chunk(c, grp=grp, g0t=g0t, gi=gi):
                def fn(ws, wb):
                    wg, wv = ws
                    s = c % 2
                    gb_ = gbuf[s]
                    gB = bf("gbuf%d" % s)
                    if gi == 0:
                        P.add("dve", lambda e: e.memset(gb_[:, 0:2], 0.0), [], [gB])
                    else:
                        cp("dve", gb_[:, 0:2], halo[:, c, :], [bf("halo")], [gB])
                    for ti in grp:
                        t0, n = TILES[ti]
                        lo = t0 - g0t
                        b1, b2 = nbank(), nbank()
                        for kc in range(KC):
                            mm(banks[b1][:, :n], wg[:, kc, :], h2T[:, kc, lo:lo + n], kc == 0, kc == KC - 1,
                               [wb, bf("h2_%d_%d" % (kc, ti))], b1)
                        for kc in range(KC):
                            mm(banks[b2][:, :n], wv[:, kc, :], h2T[:, kc, lo:lo + n], kc == 0, kc == KC - 1,
                               [wb, bf("h2_%d_%d" % (kc, ti))], b2)
                        a_ = acc_t[s]
                        aB = bf("acc%d" % s)
                        if ti < 4:
                            cp("act", gb_[:, 2 + lo:2 + lo + n], banks[b1][:, :n], [bankB[b1]], [gB])
                            act(a_[:, :n], banks[b1][:, :n], AF.Identity, [bankB[b1], bf("vecs")], [aB], bias=cb(c), scale=cw(c, 2))
                            stt(a_[:, :n], gb_[:, 1 + lo:1 + lo + n], cw(c, 1), a_[:, :n], ALU.mult, ALU.add, [gB, aB, bf("vecs")], [aB])
                            stt(a_[:, :n], gb_[:, lo:lo + n], cw(c, 0), a_[:, :n], ALU.mult, ALU.add, [gB, aB, bf("vecs")], [aB])
                            if ti == 3:
                                cp("act", convp_st[:, c, :], gb_[:, 2 + lo + n - 2:2 + lo + n], [gB], [bf("convp_st")])
                            if gi < len(GROUPS) - 1 and ti == grp[-1]:
                                cp("act", halo[:, c, :], gb_[:, 2 + lo + n - 2:2 + lo + n], [gB], [bf("halo")])
                        else:
                            g6 = gbs[s]
                            g6B = bf("gbs%d" % s)
                            if g6B not in out_keys:
                                out_keys.append(g6B)
                            dma("sp", g6[:, 0:2, :], convst_d[l][:, c * 32:(c + 1) * 32].rearrange("p (j b) -> p j b", j=2),
                                [], [g6B], g6B)
                            pg = banks[b1][:, 0:64].rearrange("p (b t) -> p t b", t=4)
                            cp("act", g6[:, 2:6, :], pg, [bankB[b1]], [g6B])
                            a3 = a_[:, 0:64].rearrange("p (t b) -> p t b", t=4)
                            act(a3, pg, AF.Identity, [bankB[b1], bf("vecs")], [aB], bias=cb(c), scale=cw(c, 2))
                            stt(a3, g6[:, 1:5, :], cw(c, 1), a3, ALU.mult, ALU.add, [g6B, aB, bf("vecs")], [aB])
                            stt(a3, g6[:, 0:4, :], cw(c, 0), a3, ALU.mult, ALU.add, [g6B, aB, bf("vecs")], [aB])
                            dma("sp", convs_d[l][:, c * 32:(c + 1) * 32].rearrange("p (j b) -> p j b", j=2), g6[:, 4:6, :],
                                [g6B], [], g6B)
                            act(ge_t[s][:, 0:64], a_[:, 0:64], AF.Gelu_apprx_tanh, [aB], [bf("ge%d" % s)])
                            tt("dve", act_c(c)[:, lo:lo + 64].rearrange("p (b t) -> p t b", t=4),
                               ge_t[s][:, 0:64].rearrange("p (t b) -> p t b", t=4),
                               banks[b2][:, 0:64].rearrange("p (b t) -> p t b", t=4), ALU.mult,
                               [bf("ge%d" % s), bankB[b2]], [bf("act%d_%d" % (c, ti))])
                            continue
                        act(ge_t[s][:, :n], a_[:, :n], AF.Gelu_apprx_tanh, [aB], [bf("ge%d" % s)])
                        tt("dve", act_c(c)[:, lo:lo + n], ge_t[s][:, :n], banks[b2][:, :n], ALU.mult,
                           [bf("ge%d" % s), bankB[b2]], [bf("act%d_%d" % (c, ti))])
                return fn

            for c in range(FC):
                job([wview(w_up_d, l, 0, D, c * 128, 128), wview(w_up_d, l, 0, D, DFF + c * 128, 128)], d_chunk(c))
                if gi == len(GROUPS) - 1 and l + 1 < depth:
                    m_job(l + 1, c)
            if gi == len(GROUPS) - 1 and l + 1 < depth:
                m_job(l + 1, 22)
                m_job(l + 1, 23)

            for ti in grp:
                def dn_chunk(mp, ti=ti, g0t=g0t):
                    def fn(ws, wb):
                        w = ws[0]
                        t0, n = TILES[ti]
                        lo = t0 - g0t
                        bk = nbank()
                        for c in range(FC):
                            mm(banks[bk][:, :n], w[:, c, :], act_c(c)[:, lo:lo + n], c == 0, c == FC - 1,
                               [wb, bf("act%d_%d" % (c, ti))], bk)
                        out_norm_chunk(l, 1, ti, mp, bk, ytile_d, [bf("ytd%d" % c) for c in range(KC)])
                    return fn
                for mp in range(KC):
                    job([wview(w_down_d, l, 0, DFF, mp * 128, 128)], dn_chunk(mp))

        def l_out():
            dma("sp", convp_d[l], convp_st[:].rearrange("p c j -> p (c j)"), [bf("convp_st")], [], bf("convp_st"))
            if bf("convp_st") not in out_keys:
                out_keys.append(bf("convp_st"))

        job(None, l_out)
        job(None, P.barrier)

    for l in range(depth):
        layer(l)

    def final_out():
        for c in range(KC):
            dma("sp", yT_d[c * 128:(c + 1) * 128, :], xT[:, c, :], [xB[c][t] for t in range(5)], [], ystore)
        out_keys.append(ystore)

    if stop is not None:
        order = ['start', 'M', 'N1', 'A', 'B', 'BS', 'C', 'D']
        keep = set(order[:order.index(stop) + 1])
        jobs[:] = [j for j in jobs if j[2] in keep]
    settag('final')
    job(None, final_out)
    run_jobs()

    P.finalize(nc, stack)
    finals = [(k.sem, 16 * k.cnt) for k in out_keys if k.sem is not None]
    with nc.Block() as block:
        @block.tensor
        def _(e):
            P.emit("pe", e)

        @block.scalar
        def _(e):
            P.emit("act", e)

        @block.vector
        def _(e):
            P.emit("dve", e)

        @block.gpsimd
        def _(e):
            P.emit("pool", e)

        @block.sync
        def _(e):
            P.emit("sp", e, finals)
    stack.close()
    return nc, P


def _consts():
    c = np.zeros((128, NCON), np.float32)
    p = np.arange(128)[:, None]
    q = np.arange(128)[None, :]
    c[:, C_ID:C_ID + 128] = (p == q)
    c[:, C_MPREV:C_MPREV + 128] = (p >= q)
    c[:, C_MOWN:C_MOWN + 128] = (p <= q)
    k = np.arange(64)[:, None]
    qq = np.arange(64)[None, :]
    same = (k // 4) == (qq // 4)
    c[:64, C_SSP:C_SSP + 64] = same & ((k % 4) <= (qq % 4))
    c[:64, C_SNEW0:C_SNEW0 + 64] = same & ((k % 4) <= (qq % 4))
    c[:64, C_SNEW1:C_SNEW1 + 64] = same & ((k % 4) == (qq % 4))
    col = np.arange(16)[None, :]
    c[:, C_MC0:C_MC0 + 16] = (p >= (col % 4))
    col = np.arange(64)[None, :]
    c[:, C_MC1:C_MC1 + 64] = ((col // 16) == (col % 4)) & (p >= 0)
    return c


_PROG = {}


def _get_prog(depth=DEPTH):
    if depth not in _PROG:
        _PROG[depth] = build(depth)[0]
    return _PROG[depth]


def make_in_maps(inputs, cores):
    f = lambda a: np.ascontiguousarray(a, dtype=np.float32)
    ada_b, norm_g = inputs["ada_b"], inputs["norm_g"]
    conv_w, conv_b = inputs["conv_w"], inputs["conv_b"]
    vecs = np.concatenate([
        ada_b.reshape(4, 48, 128).transpose(2, 0, 1).reshape(128, -1),
        norm_g.reshape(4, 4, 8, 128).transpose(3, 0, 1, 2).reshape(128, -1),
        conv_w.reshape(4, 3, 22, 128).transpose(3, 0, 2, 1).reshape(128, -1),
        conv_b.reshape(4, 22, 128).transpose(2, 0, 1).reshape(128, -1)], axis=1)
    assert vecs.shape == (128, NV)
    wsp = inputs["w_spatial"]
    wspT = wsp.transpose(0, 3, 1, 2).reshape(4, 128, 512)
    w4 = wsp[:, :, :4, :4].transpose(0, 3, 1, 2)
    wsps = np.broadcast_to(w4[:, None, :, :, None, :], (4, 16, 4, 4, 16, 4)).reshape(4, 64, 256)
    bs = inputs["b_spatial"]
    lnv = np.concatenate([inputs["ln_v_g"], inputs["ln_v_b"], bs.reshape(4, 512),
                          np.broadcast_to(bs[:, :, None, :4], (4, 4, 16, 4)).reshape(4, 256)], axis=1)
    lnv = np.broadcast_to(lnv[:, None, :], (4, 128, 1792))
    shared = {
        "ada_w": f(inputs["ada_w"]), "w_in": f(inputs["w_in"]), "w_a2d": f(inputs["w_a2d"]),
        "w_b2d": f(inputs["w_b2d"]), "w_out": f(inputs["w_out"]), "w_up": f(inputs["w_up"]),
        "w_down": f(inputs["w_down"]), "vecs": f(vecs), "lnv": f(lnv), "wspT": f(wspT), "wsps": f(wsps),
        "consts": _consts(),
    }
    maps = []
    for i in cores:
        sl = slice(16 * i, 16 * i + 16)
        xs = inputs["x_sample"][sl].reshape(64, D)
        xT = np.concatenate([inputs["x_prompt"][i].T, xs.T], axis=1)
        c = np.concatenate([inputs["c_prompt"][i][None], inputs["c_sample"][sl]], axis=0)
        cT = c.T.reshape(8, 128, 17).transpose(1, 0, 2).reshape(128, 136)
        st = inputs["state_ffn_conv"][:, sl]
        convst = st.reshape(4, 16, 2, 22, 128).transpose(0, 4, 3, 2, 1).reshape(4, 128, FC * 32)
        m = dict(shared)
        m.update({
            "xT": f(xT), "cT": f(cT), "convst": f(convst),
            "cache0": f(inputs["cache_swa0"][:, sl].reshape(4, 16, 128, 512)),
            "cache1": f(inputs["cache_swa1"][:, sl].reshape(4, 16, 512, 512)),
            "cache2": f(inputs["cache_swa2"][:, sl].reshape(4, 16, 2048, 512)),
        })
        maps.append(m)
    return maps


def assemble(results, cores, ncores_total=8):
    nb = ncores_total
    y_p = np.zeros((nb, NP_, D), np.float32)
    y_s = np.zeros((16 * nb, 4, D), np.float32)
    swa_p = [np.zeros((4, nb, k, 2, 4, 64), np.float32) for k in (128, 512, 2048)]
    conv_p = np.zeros((4, nb, 2, DFF), np.float32)
    swa_s = [np.zeros((4, 16 * nb, 4, 2, 4, 64), np.float32) for _ in range(3)]
    conv_s = np.zeros((4, 16 * nb, 2, DFF), np.float32)
    vn_s = np.zeros((4, 16 * nb, 4, 512), np.float32)
    for r, i in zip(results, cores):
        sl = slice(16 * i, 16 * i + 16)
        yT = np.asarray(r["yT"])
        y_p[i] = yT[:, :NP_].T
        y_s[sl] = yT[:, NP_:].T.reshape(16, 4, D)
        for g, k in enumerate((128, 512, 2048)):
            swa_p[g][:, i] = np.asarray(r["kvp%d" % g]).reshape(4, k, 2, 4, 64)
            swa_s[g][:, sl] = np.asarray(r["kvs"])[:, g].reshape(4, 16, 4, 2, 4, 64)
        conv_p[:, i] = np.asarray(r["convp"]).reshape(4, 128, 22, 2).transpose(0, 3, 2, 1).reshape(4, 2, DFF)
        conv_s[:, sl] = np.asarray(r["convs"]).reshape(4, 128, 22, 2, 16).transpose(0, 4, 3, 2, 1).reshape(4, 16, 2, DFF)
        vn_s[:, sl] = np.asarray(r["vns"]).reshape(4, 16, 4, 512)
    return (y_p, y_s, swa_p[0], swa_p[1], swa_p[2], conv_p, swa_s[0], swa_s[1], swa_s[2], conv_s, vn_s)


def kernel(**inputs):
    nc = _get_prog(DEPTH)
    cores = list(range(8))
    maps = make_in_maps(inputs, cores)
    res = run_bass_kernel_spmd(nc, maps, core_ids=cores)
    return assemble(res.results, cores)
```
